# Optimizing a Trainium2 kernel written in Bass

```python
import math
import jax
import jax.numpy as jnp
from jax import lax
import numpy as np


D_MODEL = 2048
BATCH = 1
SEQ = 8192
DEPTH = 4

EPS = 1e-6
D_FF = 5632
D_MIX = 2 * D_MODEL
CONV_WIDTH = 5
SSD_HEADS = 32
SSD_HEAD_DIM = 64
SSD_WIDTH = SSD_HEADS * SSD_HEAD_DIM
SSD_GROUPS = 4
SSD_STATE = 128
SSD_CHUNK = 128
SSD_BC = SSD_GROUPS * SSD_STATE
SSD_CONV_CH = SSD_WIDTH + 2 * SSD_BC
SSD_IN = SSD_WIDTH + SSD_CONV_CH + 2 * SSD_HEADS
GDN_HEADS = 16
GDN_HEAD_DIM = 128
GDN_WIDTH = GDN_HEADS * GDN_HEAD_DIM
GDN_CHUNK = 64
GDN_CONV_CH = 3 * GDN_WIDTH
GDN_IN = GDN_CONV_CH + GDN_WIDTH + 4 * GDN_HEADS
D_IN_PROJ = SSD_IN + GDN_IN

kernel_name = 'hybrid_ssd_gdn_macaron_encoder'


def rms_norm(x, g):
    xf = x.astype(jnp.float32)
    xf = xf * lax.rsqrt(jnp.mean(xf * xf, axis=-1, keepdims=True) + EPS)
    return (xf * g.astype(jnp.float32)).astype(x.dtype)


def group_rms(x, n_groups):
    shp = x.shape
    xg = x.reshape(shp[:-1] + (n_groups, shp[-1] // n_groups))
    xg = xg * lax.rsqrt(jnp.mean(xg * xg, axis=-1, keepdims=True) + EPS)
    return xg.reshape(shp)


def l2_normalise(x):
    return x * lax.rsqrt(jnp.sum(x * x, axis=-1, keepdims=True) + EPS)


def swiglu(h, w_gu, w_down):
    gate, up = jnp.split(h @ w_gu, 2, axis=-1)
    return (jax.nn.silu(gate) * up) @ w_down


def centred_dwconv(x, w):
    pad = w.shape[0] // 2
    return lax.conv_general_dilated(
        x, w[:, None, :].astype(x.dtype), window_strides=(1,), padding=[(pad, pad)],
        dimension_numbers=('NWC', 'WIO', 'NWC'), feature_group_count=x.shape[-1])


def flip_seq(t):
    return jnp.flip(t, axis=1)


def tril_segsum(a):
    t = a.shape[-1]
    cs = jnp.cumsum(a, axis=-1)
    diff = cs[..., :, None] - cs[..., None, :]
    return jnp.where(jnp.tril(jnp.ones((t, t), dtype=bool)), diff, -jnp.inf)


def ssd_scan(x, dt, a_neg, bm, cm):
    b, l, h, p = x.shape
    g, n = bm.shape[-2:]
    r = h // g
    q = SSD_CHUNK
    c = l // q
    x_dt = (x * dt[..., None]).reshape(b, c, q, g, r, p)
    a = (dt * a_neg).reshape(b, c, q, g, r).transpose(0, 3, 4, 1, 2)
    bm = bm.reshape(b, c, q, g, n)
    cm = cm.reshape(b, c, q, g, n)
    a_cs = jnp.cumsum(a, axis=-1)
    decay_in = jnp.exp(tril_segsum(a))
    cb = jnp.einsum('bclgn,bcsgn->bcgls', cm, bm)
    y_diag = jnp.einsum('bcgls,bgrcls,bcsgrp->bclgrp', cb, decay_in, x_dt)
    decay_to_end = jnp.exp(a_cs[..., -1:] - a_cs)
    states = jnp.einsum('bcsgn,bgrcs,bcsgrp->bcgrpn', bm, decay_to_end, x_dt)
    chunk_decay = jnp.exp(a_cs[..., -1])

    def step(carry, inp):
        st, dec = inp
        return carry * dec[..., None, None] + st, carry

    h0 = jnp.zeros_like(states[:, 0])
    _, h_prev = lax.scan(step, h0, (jnp.moveaxis(states, 1, 0), jnp.moveaxis(chunk_decay, 3, 0)))
    h_prev = jnp.moveaxis(h_prev, 0, 1)
    y_off = jnp.einsum('bclgn,bcgrpn,bgrcl->bclgrp', cm, h_prev, jnp.exp(a_cs))
    return (y_diag + y_off).reshape(b, l, h, p)


def gated_delta_chunked(q, k, v, g, beta):
    b, l, h, dk = q.shape
    dv = v.shape[-1]
    cs = GDN_CHUNK
    nc = l // cs

    def to_chunks(t):
        return t.reshape(b, nc, cs, h, -1).transpose(0, 3, 1, 2, 4)

    q = to_chunks(q) * (dk ** -0.5)
    k = to_chunks(k)
    v = to_chunks(v)
    beta = beta.reshape(b, nc, cs, h).transpose(0, 3, 1, 2)
    g_cs = jnp.cumsum(g.reshape(b, nc, cs, h).transpose(0, 3, 1, 2), axis=-1)
    decay = jnp.exp(tril_segsum_from_cumsum(g_cs))
    strict = jnp.tril(jnp.ones((cs, cs), dtype=bool), -1)
    k_beta = k * beta[..., None]
    a_strict = jnp.where(strict, jnp.einsum('bhnid,bhnjd->bhnij', k_beta, k) * decay, 0.0)
    rhs = jnp.concatenate([v * beta[..., None], k_beta * jnp.exp(g_cs)[..., None]], axis=-1)
    sol = lax.linalg.triangular_solve(a_strict, rhs, left_side=True, lower=True, unit_diagonal=True)
    u, w = sol[..., :dv], sol[..., dv:]
    attn_qk = jnp.einsum('bhnid,bhnjd->bhnij', q, k) * decay
    q_decay = q * jnp.exp(g_cs)[..., None]
    k_to_end = k * jnp.exp(g_cs[..., -1:] - g_cs)[..., None]
    last_decay = jnp.exp(g_cs[..., -1])

    def step(s, inp):
        u_c, w_c, qd_c, aq_c, ke_c, ld_c = inp
        v_new = u_c - jnp.einsum('bhid,bhde->bhie', w_c, s)
        o = jnp.einsum('bhid,bhde->bhie', qd_c, s) + jnp.einsum('bhij,bhje->bhie', aq_c, v_new)
        s = s * ld_c[..., None, None] + jnp.einsum('bhid,bhie->bhde', ke_c, v_new)
        return s, o

    s0 = jnp.zeros((b, h, dk, dv), q.dtype)
    xs = (jnp.moveaxis(u, 2, 0), jnp.moveaxis(w, 2, 0), jnp.moveaxis(q_decay, 2, 0),
          jnp.moveaxis(attn_qk, 2, 0), jnp.moveaxis(k_to_end, 2, 0), jnp.moveaxis(last_decay, 2, 0))
    _, o = lax.scan(step, s0, xs)
    return o.transpose(1, 0, 3, 2, 4).reshape(b, l, h, dv)


def tril_segsum_from_cumsum(cs_vals):
    t = cs_vals.shape[-1]
    diff = cs_vals[..., :, None] - cs_vals[..., None, :]
    return jnp.where(jnp.tril(jnp.ones((t, t), dtype=bool)), diff, -jnp.inf)


def hybrid_mixer(h, w_in, ssd_conv_w, ssd_conv_b, ssd_dt_bias, ssd_a_log, ssd_d, ssd_norm_g,
                 gdn_conv_w, gdn_dt_bias, gdn_a_log, gdn_norm_g, w_out):
    b, l, _ = h.shape
    f32 = jnp.float32
    proj = h @ w_in
    ssd_in, gdn_in = proj[..., :SSD_IN], proj[..., SSD_IN:]

    z_s, xbc, dt_raw = jnp.split(ssd_in, [SSD_WIDTH, SSD_WIDTH + SSD_CONV_CH], axis=-1)
    xbc = jax.nn.silu(centred_dwconv(xbc, ssd_conv_w) + ssd_conv_b.astype(xbc.dtype)).astype(f32)
    xs, bm, cm = jnp.split(xbc, [SSD_WIDTH, SSD_WIDTH + SSD_BC], axis=-1)
    xs = xs.reshape(b, l, SSD_HEADS, SSD_HEAD_DIM)
    bm = bm.reshape(b, l, SSD_GROUPS, SSD_STATE)
    cm = cm.reshape(b, l, SSD_GROUPS, SSD_STATE)
    dt = jax.nn.softplus(dt_raw.astype(f32).reshape(b, l, 2, SSD_HEADS) + ssd_dt_bias.astype(f32))
    a_neg = -jnp.exp(ssd_a_log.astype(f32))
    y_fwd = ssd_scan(xs, dt[:, :, 0], a_neg[0], bm, cm)
    y_bwd = flip_seq(ssd_scan(flip_seq(xs), flip_seq(dt[:, :, 1]), a_neg[1], flip_seq(bm), flip_seq(cm)))
    y = (y_fwd + y_bwd + ssd_d.astype(f32)[:, None] * xs).reshape(b, l, SSD_WIDTH)
    y_ssd = group_rms(y * jax.nn.silu(z_s.astype(f32)), SSD_GROUPS) * ssd_norm_g.astype(f32)

    qkv, z_g, ab = jnp.split(gdn_in, [GDN_CONV_CH, GDN_CONV_CH + GDN_WIDTH], axis=-1)
    qkv = jax.nn.silu(centred_dwconv(qkv, gdn_conv_w)).astype(f32)
    q, k, v = jnp.split(qkv, 3, axis=-1)
    q = l2_normalise(q.reshape(b, l, GDN_HEADS, GDN_HEAD_DIM))
    k = l2_normalise(k.reshape(b, l, GDN_HEADS, GDN_HEAD_DIM))
    v = v.reshape(b, l, GDN_HEADS, GDN_HEAD_DIM)
    ab = ab.astype(f32).reshape(b, l, 2, 2, GDN_HEADS)
    g = -jnp.exp(gdn_a_log.astype(f32)) * jax.nn.softplus(ab[:, :, 0] + gdn_dt_bias.astype(f32))
    beta = jax.nn.sigmoid(ab[:, :, 1])
    o_fwd = gated_delta_chunked(q, k, v, g[:, :, 0], beta[:, :, 0])
    o_bwd = flip_seq(gated_delta_chunked(flip_seq(q), flip_seq(k), flip_seq(v),
                                         flip_seq(g[:, :, 1]), flip_seq(beta[:, :, 1])))
    o = group_rms(o_fwd + o_bwd, 1) * gdn_norm_g.astype(f32)
    o = o * jax.nn.silu(z_g.astype(f32).reshape(b, l, GDN_HEADS, GDN_HEAD_DIM))
    y_gdn = o.reshape(b, l, GDN_WIDTH)

    y_all = jnp.concatenate([y_ssd, y_gdn], axis=-1).astype(h.dtype)
    return y_all @ w_out


def setup_inputs(seed: int = 0) -> dict:
    key = jax.random.key(seed)
    ks = jax.random.split(key, 23)
    f32 = jnp.float32

    def dense(k, fan_in, fan_out):
        return jax.random.normal(k, (DEPTH, fan_in, fan_out), f32) * fan_in ** -0.5

    def gain(k, d):
        return 1.0 + 0.02 * jax.random.normal(k, (DEPTH, d), f32)

    def a_log(k, nh):
        return jnp.log(jax.random.uniform(k, (DEPTH, 2, nh), f32, 1.0, 16.0))

    def dt_bias(k, nh):
        dt = jnp.exp(jax.random.uniform(k, (DEPTH, 2, nh), f32, math.log(1e-3), math.log(1e-1)))
        return dt + jnp.log(-jnp.expm1(-dt))

    return {
        'x': jax.random.normal(ks[0], (BATCH, SEQ, D_MODEL), f32),
        'ffn1_pre_g': gain(ks[1], D_MODEL),
        'ffn1_w_gu': dense(ks[2], D_MODEL, 2 * D_FF),
        'ffn1_w_down': dense(ks[3], D_FF, D_MODEL),
        'ffn1_post_g': gain(ks[4], D_MODEL),
        'mix_pre_g': gain(ks[5], D_MODEL),
        'w_in': dense(ks[6], D_MODEL, D_IN_PROJ),
        'ssd_conv_w': jax.random.normal(ks[7], (DEPTH, CONV_WIDTH, SSD_CONV_CH), f32) * CONV_WIDTH ** -0.5,
        'ssd_conv_b': 0.02 * jax.random.normal(ks[8], (DEPTH, SSD_CONV_CH), f32),
        'ssd_dt_bias': dt_bias(ks[9], SSD_HEADS),
        'ssd_a_log': a_log(ks[10], SSD_HEADS),
        'ssd_d': 1.0 + 0.1 * jax.random.normal(ks[11], (DEPTH, SSD_HEADS), f32),
        'ssd_norm_g': gain(ks[12], SSD_WIDTH),
        'gdn_conv_w': jax.random.normal(ks[13], (DEPTH, CONV_WIDTH, GDN_CONV_CH), f32) * CONV_WIDTH ** -0.5,
        'gdn_dt_bias': dt_bias(ks[14], GDN_HEADS),
        'gdn_a_log': a_log(ks[15], GDN_HEADS),
        'gdn_norm_g': gain(ks[16], GDN_HEAD_DIM),
        'w_out': dense(ks[17], D_MIX, D_MODEL),
        'mix_post_g': gain(ks[18], D_MODEL),
        'ffn2_pre_g': gain(ks[19], D_MODEL),
        'ffn2_w_gu': dense(ks[20], D_MODEL, 2 * D_FF),
        'ffn2_w_down': dense(ks[21], D_FF, D_MODEL),
        'ffn2_post_g': gain(ks[22], D_MODEL),
    }


def reference(x, ffn1_pre_g, ffn1_w_gu, ffn1_w_down, ffn1_post_g, mix_pre_g, w_in,
              ssd_conv_w, ssd_conv_b, ssd_dt_bias, ssd_a_log, ssd_d, ssd_norm_g,
              gdn_conv_w, gdn_dt_bias, gdn_a_log, gdn_norm_g, w_out, mix_post_g,
              ffn2_pre_g, ffn2_w_gu, ffn2_w_down, ffn2_post_g):
    for i in range(DEPTH):
        f = swiglu(rms_norm(x, ffn1_pre_g[i]), ffn1_w_gu[i], ffn1_w_down[i])
        x = x + 0.5 * rms_norm(f, ffn1_post_g[i])
        m = hybrid_mixer(rms_norm(x, mix_pre_g[i]), w_in[i],
                         ssd_conv_w[i], ssd_conv_b[i], ssd_dt_bias[i], ssd_a_log[i], ssd_d[i], ssd_norm_g[i],
                         gdn_conv_w[i], gdn_dt_bias[i], gdn_a_log[i], gdn_norm_g[i], w_out[i])
        x = x + rms_norm(m, mix_post_g[i])
        f = swiglu(rms_norm(x, ffn2_pre_g[i]), ffn2_w_gu[i], ffn2_w_down[i])
        x = x + 0.5 * rms_norm(f, ffn2_post_g[i])
    return x
```

```python
import numpy as np
import concourse.bass as bass
import concourse.mybir as mybir
from concourse.bass_utils import run_bass_kernel_spmd
from contextlib import ExitStack

F32 = mybir.dt.float32
BF16 = mybir.dt.bfloat16
AF = mybir.ActivationFunctionType
ALU = mybir.AluOpType

D = 2048
KC = D // 128
NT = 1024
SEQ = 8192
import os
DFF = int(os.environ.get('DFF', 5632))
FC = DFF // 128
EPS = 1e-6
NCORES = 8


class Buf:
    __slots__ = ("w", "rs", "name")

    def __init__(self, name=""):
        self.w = None
        self.rs = []
        self.name = name


class MK:
    def __init__(self, nc, es):
        self.nc = nc
        self.es = es
        self.eng = {"pe": nc.tensor, "act": nc.scalar, "dve": nc.vector, "pool": nc.gpsimd, "sp": nc.sync}
        self.sem = {k: es.enter_context(nc.semaphore("s_" + k)) for k in self.eng}
        self.cnt = {k: 0 for k in self.eng}
        self.seen = {k: {} for k in self.eng}
        self.dsems = {}
        self.nbuf = 0

    def buf(self, name=""):
        self.nbuf += 1
        return Buf(name)

    def _wait(self, e, tok):
        if tok is None:
            return
        kind, sem, val, key = tok
        if kind == "dma":
            val = self.dsems[key[2:]][1]
        if e == "pe" and key == "pe":
            return
        if self.seen[e].get(key, 0) >= val:
            return
        self.eng[e].wait_ge(sem, val)
        self.seen[e][key] = val

    def _deps(self, e, reads, writes):
        for b in reads:
            self._wait(e, b.w)
        for b in writes:
            self._wait(e, b.w)
            for t in b.rs:
                self._wait(e, t)

    def _record(self, tok, reads, writes):
        for b in reads:
            b.rs.append(tok)
            if len(b.rs) > 64:
                last = {}
                for t in b.rs:
                    last[t[3]] = t
                b.rs = list(last.values())
        for b in writes:
            b.w = tok
            b.rs = []

    def op(self, e, fn, reads=(), writes=(), xreads=()):
        writes = list(writes) + list(xreads)
        self._deps(e, reads, writes)
        ins = fn(self.eng[e])
        self.cnt[e] += 1
        ins.then_inc(self.sem[e], 1)
        tok = ("eng", self.sem[e], self.cnt[e], e)
        self._record(tok, reads, writes)
        return ins

    def dsem(self, name):
        if name not in self.dsems:
            self.dsems[name] = [self.es.enter_context(self.nc.semaphore("d_" + name)), 0]
        return self.dsems[name]

    def dma(self, q, out, in_, semname, reads=(), writes=(), **kw):
        self._deps(q, reads, writes)
        s = self.dsem(semname)
        ins = self.eng[q].dma_start(out=out, in_=in_, **kw)
        s[1] += 16
        ins.then_inc(s[0], 16)
        tok = ("dma", s[0], s[1], "d_" + semname)
        self._record(tok, reads, writes)
        return ins

    def wait_all(self, e):
        for k in self.eng:
            if self.cnt[k] > 0:
                self._wait(e, ("eng", self.sem[k], self.cnt[k], k))
        for name, s in self.dsems.items():
            if s[1] > 0:
                self._wait(e, ("dma", s[0], s[1], "d_" + name))

    def barrier(self, engines=("pe", "act", "dve", "pool", "sp")):
        for e in engines:
            self.wait_all(e)


class Env:
    def __init__(self, nc, mk, es, consts_dram):
        self.nc, self.mk, self.es = nc, mk, es
        self.pb = [es.enter_context(nc.psum_tensor(f"pb{i}", [128, 512], F32)) for i in range(8)]
        self.bpb = [mk.buf(f"pb{i}") for i in range(8)]
        self.cf = es.enter_context(nc.sbuf_tensor("constf", [128, NCONST * 128], F32))
        self.cb = es.enter_context(nc.sbuf_tensor("constb", [128, NCONST * 128], BF16))
        self.onesf = es.enter_context(nc.sbuf_tensor("onesf", [128, 128], F32))
        self.onesb = es.enter_context(nc.sbuf_tensor("onesb", [128, 128], BF16))
        self.bconst = mk.buf("const")
        mk.dma("sp", self.cf[:], consts_dram, "const", writes=[self.bconst])
        mk.op("dve", lambda e: e.tensor_copy(out=self.cb[:], in_=self.cf[:]), reads=[self.bconst], writes=[self.bconst])
        mk.op("dve", lambda e: e.memset(self.onesf[:], 1.0), writes=[self.bconst])
        mk.op("dve", lambda e: e.memset(self.onesb[:], 1.0), writes=[self.bconst])
        self.epsc = es.enter_context(nc.sbuf_tensor("epsc", [128, 1], F32))
        mk.op("dve", lambda e: e.memset(self.epsc[:], EPS), writes=[self.bconst])
        self.identf = self.cf[:, 0:128]
        self.identb = self.cb[:, 0:128]

    def cF(self, i):
        return self.cf[:, i * 128:(i + 1) * 128]

    def cB(self, i):
        return self.cb[:, i * 128:(i + 1) * 128]

    def sb(self, es, name, shape, dt):
        self.uid = getattr(self, "uid", 0) + 1
        return es.enter_context(self.nc.sbuf_tensor(f"{name}_u{self.uid}", shape, dt))


NCONST = 12
C_ID, C_U, C_UT, C_L, C_LT, C_U64, C_UT64, C_L64, C_LT64, C_BD, C_H0, C_H1 = range(12)


def make_consts():
    i = np.arange(128)
    m, l = i[:, None], i[None, :]
    same = (m // 64) == (l // 64)
    mats = [m == l, m <= l, m >= l, m > l, m < l,
            (m <= l) & same, (m >= l) & same, (m > l) & same, (m < l) & same, same,
            (m < 64) & (l >= 0), (m >= 64) & (l >= 0)]
    return np.concatenate(mats, axis=1).astype(np.float32)


def rstd_from_ss(env, es, ss_banks, ntok, name, n=D, pre=None):
    mk = env.mk
    if pre is None:
        rs = env.sb(es, name + "_rs", [128, ntok], F32)
        rstd = env.sb(es, name + "_rstd", [128, ntok], F32)
        brs, brstd = mk.buf(), mk.buf()
    else:
        rs, rstd, brs, brstd = pre
    for h, bi in enumerate(ss_banks):
        sl = slice(h * 512, (h + 1) * 512)
        mk.op("act", lambda e: e.activation(out=rs[:, sl], in_=env.pb[bi][:], func=AF.Sqrt, bias=env.epsc[:, 0:1], scale=1.0 / n),
              reads=[env.bconst], xreads=[env.bpb[bi]], writes=[brs])
    mk.op("dve", lambda e: e.reciprocal(out=rstd[:], in_=rs[:]), reads=[brs], writes=[brstd])
    return rstd, brstd


def phase_prenorm(env, es_out, x_src, gcol, bg, hT, hB):
    mk, nc = env.mk, env.nc
    with ExitStack() as es:
        NR = 4
        xr = [env.sb(es, f"xr{i}", [128, NT], F32) for i in range(NR)]
        bx = [mk.buf() for _ in range(NR)]
        sq = [env.sb(es, f"sq{i}", [128, NT], F32) for i in range(2)]
        bsq = [mk.buf() for _ in range(2)]

        def load(i):
            c = i % KC
            mk.dma("sp", xr[i % NR][:], x_src[:, c, :], f"xr{i % NR}", writes=[bx[i % NR]])

        for i in range(NR - 1):
            load(i)
        rstd = brstd = None
        for i in range(2 * KC):
            if i + NR - 1 < 2 * KC:
                load(i + NR - 1)
            c, r = i % KC, i % NR
            if i < KC:
                s = c % 2
                mk.op("act", lambda e: e.activation(out=sq[s][:], in_=xr[r][:], func=AF.Square), reads=[bx[r]], writes=[bsq[s]])
                for h in range(NT // 512):
                    mk.op("pe", lambda e: e.matmul(env.pb[6 + h][:], lhsT=env.onesf[:], rhs=sq[s][:, h * 512:(h + 1) * 512],
                                                   start=(c == 0), stop=(c == KC - 1)),
                          reads=[bsq[s], env.bconst], writes=[env.bpb[6 + h]])
            else:
                if rstd is None:
                    rstd, brstd = rstd_from_ss(env, es, [6, 7], NT, "pre")
                mk.op("dve", lambda e: e.scalar_tensor_tensor(out=hT[:, c, :], in0=xr[r][:], scalar=gcol[:, c:c + 1], in1=rstd[:],
                                                              op0=ALU.mult, op1=ALU.mult),
                      reads=[bx[r], brstd, bg], writes=[hB[c]])
        mk.barrier()


def phase_postnorm_update(env, es, zT, zB, ss_banks, x_src, x_dst, bxdst, name):
    mk = env.mk
    rstd, brstd = rstd_from_ss(env, es, ss_banks, NT, name)
    xin = [env.sb(es, f"{name}_xin{i}", [128, NT], F32) for i in range(3)]
    tmp = [env.sb(es, f"{name}_tmp{i}", [128, NT], F32) for i in range(2)]
    bxin = [mk.buf() for _ in range(3)]
    btmp = [mk.buf() for _ in range(2)]
    for c in range(KC):
        s3, s2 = c % 3, c % 2
        mk.dma("sp", xin[s3][:], x_src[:, c, :], f"{name}_xin{s3}", writes=[bxin[s3]])
        mk.op("dve", lambda e: e.tensor_tensor(out=tmp[s2][:], in0=zT[:, c, :], in1=rstd[:], op=ALU.mult),
              reads=[zB[c], brstd], writes=[btmp[s2]])
        mk.op("dve", lambda e: e.tensor_tensor(out=xin[s3][:], in0=xin[s3][:], in1=tmp[s2][:], op=ALU.add),
              reads=[btmp[s2], bxin[s3]], writes=[bxin[s3]])
        mk.dma("sp", x_dst[:, c, :], xin[s3][:], f"{name}_xout", reads=[bxin[s3]], writes=[bxdst])


def proj_accum_post(env, es, nk, wtile_src, rhsT, rhsB, gcol, bg, x_src, x_dst, bxdst, name, wbufs=2):
    mk = env.mk
    zT = env.sb(es, name + "_zT", [128, KC, NT], BF16)
    zB = [mk.buf() for _ in range(KC)]
    wt = [env.sb(es, f"{name}_w{i}", [128, nk, 128], BF16) for i in range(wbufs)]
    bw = [mk.buf() for _ in range(wbufs)]
    fsq = [env.sb(es, f"{name}_fsq{i}", [128, 512], F32) for i in range(2)]
    bfsq = [mk.buf() for _ in range(2)]
    nh = NT // 512

    def load(c):
        s = c % wbufs
        mk.dma("pool", wt[s][:], wtile_src(c), f"{name}_w{s}", writes=[bw[s]])

    for c in range(min(wbufs - 1, KC)):
        load(c)
    it = 0
    for c in range(KC):
        if c + wbufs - 1 < KC:
            load(c + wbufs - 1)
        s = c % wbufs
        for h in range(nh):
            bi = it % 4
            sl = slice(h * 512, (h + 1) * 512)
            for k in range(nk):
                mk.op("pe", lambda e: e.matmul(env.pb[bi][:], lhsT=wt[s][:, k, :], rhs=rhsT[:, k, sl], start=(k == 0), stop=(k == nk - 1)),
                      reads=[bw[s], rhsB[k]], writes=[env.bpb[bi]])
            q = it % 2
            mk.op("act", lambda e: e.activation(out=fsq[q][:], in_=env.pb[bi][:], func=AF.Square), xreads=[env.bpb[bi]], writes=[bfsq[q]])
            mk.op("dve", lambda e: e.tensor_scalar(out=zT[:, c, sl], in0=env.pb[bi][:], scalar1=gcol[:, c:c + 1], scalar2=None, op0=ALU.mult),
                  reads=[bg], xreads=[env.bpb[bi]], writes=[zB[c]])
            if os.environ.get("K1") is None or c == 0:
                mk.op("pe", lambda e: e.matmul(env.pb[4 + h][:], lhsT=env.onesf[:], rhs=fsq[q][:], start=(c == 0), stop=(c == KC - 1)),
                      reads=[bfsq[q], env.bconst], writes=[env.bpb[4 + h]])
            it += 1
    if os.environ.get("K2") is None:
        phase_postnorm_update(env, es, zT, zB, [4 + h for h in range(nh)], x_src, x_dst, bxdst, name)


def phase_ffn(env, x_src, x_dst, bxdst, w_gu, w_down, gpre, gpost_half, bg):
    mk, nc = env.mk, env.nc
    with ExitStack() as es:
        actT = env.sb(es, "actT", [128, FC, NT], BF16)
        aB = [mk.buf() for _ in range(FC)]
        with ExitStack() as es2:
            hT = env.sb(es2, "hT", [128, KC, NT], BF16)
            hB = [mk.buf() for _ in range(KC)]
            phase_prenorm(env, es2, x_src, gpre, bg, hT, hB)
            if os.environ.get("STAGE") == "1":
                return
            NB = 3
            wg = [env.sb(es2, f"wg{i}", [128, KC, 128], BF16) for i in range(NB)]
            wu = [env.sb(es2, f"wu{i}", [128, KC, 128], BF16) for i in range(NB)]
            bwt = [mk.buf() for _ in range(NB)]
            sg = [env.sb(es2, f"sg{i}", [128, 512], F32) for i in range(2)]
            bsg = [mk.buf() for _ in range(2)]

            def load(j):
                s = j % NB
                mk.dma("pool", wg[s][:], w_gu[:, j * 128:(j + 1) * 128].rearrange("(k p) c -> p k c", p=128), f"wgu{s}", writes=[bwt[s]])
                mk.dma("pool", wu[s][:], w_gu[:, DFF + j * 128:DFF + (j + 1) * 128].rearrange("(k p) c -> p k c", p=128), f"wgu{s}", writes=[bwt[s]])

            for j in range(NB - 1):
                load(j)
            it = 0
            for j in range(FC):
                if j + NB - 1 < FC:
                    load(j + NB - 1)
                s = j % NB
                for h in range(NT // 512):
                    sl = slice(h * 512, (h + 1) * 512)
                    bgi, bui = 2 * (it % 2), 2 * (it % 2) + 1
                    for k in range(KC):
                        mk.op("pe", lambda e: e.matmul(env.pb[bgi][:], lhsT=wg[s][:, k, :], rhs=hT[:, k, sl], start=(k == 0), stop=(k == KC - 1)),
                              reads=[bwt[s], hB[k]], writes=[env.bpb[bgi]])
                    for k in range(KC):
                        mk.op("pe", lambda e: e.matmul(env.pb[bui][:], lhsT=wu[s][:, k, :], rhs=hT[:, k, sl], start=(k == 0), stop=(k == KC - 1)),
                              reads=[bwt[s], hB[k]], writes=[env.bpb[bui]])
                    q = it % 2
                    mk.op("act", lambda e: e.activation(out=sg[q][:], in_=env.pb[bgi][:], func=AF.Silu), xreads=[env.bpb[bgi]], writes=[bsg[q]])
                    mk.op("dve", lambda e: e.tensor_tensor(out=actT[:, j, sl], in0=sg[q][:], in1=env.pb[bui][:], op=ALU.mult),
                          reads=[bsg[q]], xreads=[env.bpb[bui]], writes=[aB[j]])
                    it += 1
            mk.barrier()
        if os.environ.get("STAGE") == "2":
            return
        with ExitStack() as es3:
            proj_accum_post(env, es3, FC, lambda c: w_down[:, c * 128:(c + 1) * 128].rearrange("(j p) c -> p j c", p=128),
                            actT, aB, gpost_half, bg, x_src, x_dst, bxdst, "dn")
            mk.barrier()


def phase_ma(env, x_src, gcol, bg, hT_dst, bhdst):
    mk = env.mk
    with ExitStack() as es:
        hT = env.sb(es, "hTma", [128, KC, NT], BF16)
        hB = [mk.buf() for _ in range(KC)]
        phase_prenorm(env, es, x_src, gcol, bg, hT, hB)
        for c in range(KC):
            mk.dma("sp", hT_dst[:, c, :], hT[:, c, :], "hTout", reads=[hB[c]], writes=[bhdst])
        mk.barrier()


def phase_mc(env, x_src, x_dst, bxdst, hT_src, yT_src, w_in, w_out, gs, bg, zcols=None):
    mk = env.mk
    nh = NT // 512
    with ExitStack() as es:
        yT = env.sb(es, "yT", [128, 32, NT], BF16)
        yB = [mk.buf() for _ in range(32)]
        for c in range(32):
            mk.dma("sp", yT[:, c, :], yT_src[:, c, :], "yTin", writes=[yB[c]])
        with ExitStack() as es2:
            hT = env.sb(es2, "hTmc", [128, KC, NT], BF16)
            hB = [mk.buf() for _ in range(KC)]
            for c in range(KC):
                mk.dma("sp", hT[:, c, :], hT_src[:, c, :], "hTin", writes=[hB[c]])
            wz = [env.sb(es2, f"wz{i}", [128, KC, 128], BF16) for i in range(3)]
            bwz = [mk.buf() for _ in range(3)]
            uu = [env.sb(es2, f"uu{i}", [128, NT], F32) for i in range(4)]
            buu = [mk.buf() for _ in range(4)]
            sz = [env.sb(es2, f"sz{i}", [128, NT], F32) for i in range(2)]
            bsz = [mk.buf() for _ in range(2)]
            sq = [env.sb(es2, f"sqm{i}", [128, NT], F32) for i in range(2)]
            bsq = [mk.buf() for _ in range(2)]
            if zcols is None:
                zcols = [i * 128 for i in range(16)] + [11328 + i * 128 for i in range(16)]
            pre = (env.sb(es2, "mc_rs", [128, NT], F32), env.sb(es2, "mc_rstd", [128, NT], F32), mk.buf(), mk.buf())

            def loadz(i):
                mk.dma("pool", wz[i % 3][:], w_in[:, zcols[i]:zcols[i] + 128].rearrange("(k p) c -> p k c", p=128), f"wz{i % 3}", writes=[bwz[i % 3]])

            def zproj(i):
                s = i % 3
                q = i % 2
                for h in range(nh):
                    bi = (2 * i + h) % 4
                    sl = slice(h * 512, (h + 1) * 512)
                    for k in range(KC):
                        mk.op("pe", lambda e: e.matmul(env.pb[bi][:], lhsT=wz[s][:, k, :], rhs=hT[:, k, sl], start=(k == 0), stop=(k == KC - 1)),
                              reads=[bwz[s], hB[k]], writes=[env.bpb[bi]])
                    mk.op("act", lambda e: e.activation(out=sz[q][:, sl], in_=env.pb[bi][:], func=AF.Silu), xreads=[env.bpb[bi]], writes=[bsz[q]])
                return sz[q], bsz[q]

            loadz(0)
            loadz(1)
            for gi in range(4):
                for j in range(4):
                    i = gi * 4 + j
                    if i + 2 < 32:
                        loadz(i + 2)
                    szt, bszt = zproj(i)
                    mk.op("dve", lambda e: e.tensor_tensor(out=uu[j][:], in0=yT[:, i, :], in1=szt[:], op=ALU.mult), reads=[yB[i], bszt], writes=[buu[j]])
                    q = i % 2
                    mk.op("act", lambda e: e.activation(out=sq[q][:], in_=uu[j][:], func=AF.Square), reads=[buu[j]], writes=[bsq[q]])
                    for h in range(nh):
                        mk.op("pe", lambda e: e.matmul(env.pb[6 + h][:], lhsT=env.onesf[:], rhs=sq[q][:, h * 512:(h + 1) * 512], start=(j == 0), stop=(j == 3)),
                              reads=[bsq[q], env.bconst], writes=[env.bpb[6 + h]])
                rstd, brstd = rstd_from_ss(env, es2, [6 + h for h in range(nh)], NT, "sg", n=512, pre=pre)
                for j in range(4):
                    i = gi * 4 + j
                    mk.op("dve", lambda e: e.scalar_tensor_tensor(out=yT[:, i, :], in0=uu[j][:], scalar=gs[:, i:i + 1], in1=rstd[:], op0=ALU.mult, op1=ALU.mult),
                          reads=[buu[j], brstd, bg], writes=[yB[i]])
            for hd in range(16):
                i = 16 + hd
                if i + 2 < 32:
                    loadz(i + 2)
                q = i % 2
                mk.op("act", lambda e: e.activation(out=sq[q][:], in_=yT[:, i, :], func=AF.Square), reads=[yB[i]], writes=[bsq[q]])
                for h in range(nh):
                    mk.op("pe", lambda e: e.matmul(env.pb[6 + h][:], lhsT=env.onesf[:], rhs=sq[q][:, h * 512:(h + 1) * 512], start=True, stop=True),
                          reads=[bsq[q], env.bconst], writes=[env.bpb[6 + h]])
                szt, bszt = zproj(i)
                rstd, brstd = rstd_from_ss(env, es2, [6 + h for h in range(nh)], NT, "gg", n=128, pre=pre)
                j = hd % 4
                mk.op("dve", lambda e: e.scalar_tensor_tensor(out=uu[j][:], in0=yT[:, i, :], scalar=gs[:, 16:17], in1=rstd[:], op0=ALU.mult, op1=ALU.mult),
                      reads=[yB[i], brstd, bg], writes=[buu[j]])
                mk.op("dve", lambda e: e.tensor_tensor(out=yT[:, i, :], in0=uu[j][:], in1=szt[:], op=ALU.mult), reads=[buu[j], bszt], writes=[yB[i]])
            mk.barrier()
        with ExitStack() as es4:
            proj_accum_post(env, es4, 32, lambda c: w_out[:, c * 128:(c + 1) * 128].rearrange("(j p) c -> p j c", p=128),
                            yT, yB, gs[:, 17:33], bg, x_src, x_dst, bxdst, "op")
            mk.barrier()


NBLK = SEQ // 128
NTILE = SEQ // 512
QSCALE = 128 ** -0.5


def conv_silu(env, win_ap, W, cw, ci, acc, bacc, reads, dve="dve"):
    mk = env.mk
    mk.op(dve, lambda e: e.tensor_scalar(out=acc[:, 0:W], in0=win_ap[:, 0:W], scalar1=cw[:, ci, 0:1], scalar2=None, op0=ALU.mult),
          reads=reads, writes=[bacc])
    for j in range(1, 5):
        mk.op("dve", lambda e: e.scalar_tensor_tensor(out=acc[:, 0:W], in0=win_ap[:, j:j + W], scalar=cw[:, ci, j:j + 1], in1=acc[:, 0:W],
                                                      op0=ALU.mult, op1=ALU.add),
              reads=reads, writes=[bacc])


def inproj_conv_pass(env, hTall, wmb, col0, nch, cw, bcw, cw_idx, pc, pcB, l2, small=None):
    mk, nc = env.mk, env.nc
    with ExitStack() as es:
        W = env.sb(es, "W", [128, KC, nch * 128], BF16)
        bW = mk.buf()
        for ci in range(nch):
            mk.dma("pool", W[:, :, ci * 128:(ci + 1) * 128],
                   wmb[:, col0 + ci * 128:col0 + (ci + 1) * 128].rearrange("(k p) c -> p k c", p=128), "W", writes=[bW])
        if small is not None:
            col0s, nsm, small_raw, bsmall = small
            Wsm = env.sb(es, "Wsm", [128, KC, nsm], BF16)
            mk.dma("pool", Wsm[:], wmb[:, col0s:col0s + nsm].rearrange("(k p) c -> p k c", p=128), "W", writes=[bW])
        ht = [env.sb(es, f"ht{i}", [128, KC, 512], BF16) for i in range(2)]
        bht = [mk.buf() for _ in range(2)]
        win = [[env.sb(es, f"win{p}_{ci}", [128, 516], F32) for ci in range(nch)] for p in range(2)]
        bwin = [[mk.buf() for ci in range(nch)] for p in range(2)]
        acc = [env.sb(es, f"acc{i}", [128, 512], F32) for i in range(2)]
        bacc = [mk.buf() for _ in range(2)]
        sl = [env.sb(es, f"sl{i}", [128, 512], F32) for i in range(2)]
        bsl = [mk.buf() for _ in range(2)]
        sq = env.sb(es, "sq", [128, 512], F32)
        rs = env.sb(es, "rs", [128, 512], F32)
        bsq, brs = mk.buf(), mk.buf()
        tail = env.sb(es, "tail", [128, nch, 8], F32)
        btail = mk.buf()
        mk.op("dve", lambda e: e.memset(tail[:], 0.0), writes=[btail])
        for ci in range(nch):
            mk.op("dve", lambda e: e.memset(win[0][ci][:, 0:4], 0.0), writes=[bwin[0][ci]])

        def load(T):
            mk.dma("sp", ht[T % 2][:], hTall[:, :, T * 512:(T + 1) * 512], f"ht{T % 2}", writes=[bht[T % 2]])

        it = [0]

        def epilogue(ci, win_ap, bwin_, W_, lo, dst0):
            q = it[0] % 2
            it[0] += 1
            conv_silu(env, win_ap, W_, cw, cw_idx[ci], acc[q], bacc[q], [bwin_, bcw])
            n = W_ - lo
            if l2[ci] is None:
                mk.op("act", lambda e: e.activation(out=pc[ci][:, dst0:dst0 + n], in_=acc[q][:, lo:W_], func=AF.Silu, bias=cw[:, cw_idx[ci], 5:6]),
                      reads=[bacc[q], bcw], writes=[pcB[ci]])
            else:
                mk.op("act", lambda e: e.activation(out=sl[q][:, 0:n], in_=acc[q][:, lo:W_], func=AF.Silu, bias=cw[:, cw_idx[ci], 5:6]),
                      reads=[bacc[q], bcw], writes=[bsl[q]])
                mk.op("act", lambda e: e.activation(out=sq[:, 0:n], in_=sl[q][:, 0:n], func=AF.Square), reads=[bsl[q]], writes=[bsq])
                mk.op("pe", lambda e: e.matmul(env.pb[5][:, 0:n], lhsT=env.onesf[:], rhs=sq[:, 0:n], start=True, stop=True),
                      reads=[bsq, env.bconst], writes=[env.bpb[5]])
                mk.op("act", lambda e: e.activation(out=rs[:, 0:n], in_=env.pb[5][:, 0:n], func=AF.Sqrt, bias=env.epsc[:, 0:1], scale=1.0),
                      reads=[env.bconst], xreads=[env.bpb[5]], writes=[brs])
                mk.op("dve", lambda e: e.reciprocal(out=rs[:, 0:n], in_=rs[:, 0:n]), reads=[brs], writes=[brs])
                mk.op("dve", lambda e: e.scalar_tensor_tensor(out=pc[ci][:, dst0:dst0 + n], in0=sl[q][:, 0:n], scalar=float(l2[ci]), in1=rs[:, 0:n],
                                                              op0=ALU.mult, op1=ALU.mult),
                      reads=[bsl[q], brs], writes=[pcB[ci]])

        load(0)
        for T in range(NTILE):
            if T + 1 < NTILE:
                load(T + 1)
            p = T % 2
            for ci in range(nch):
                bi = ci % 4
                for k in range(KC):
                    mk.op("pe", lambda e: e.matmul(env.pb[bi][:], lhsT=W[:, k, ci * 128:(ci + 1) * 128], rhs=ht[p][:, k, :],
                                                   start=(k == 0), stop=(k == KC - 1)),
                          reads=[bW, bht[p]], writes=[env.bpb[bi]])
                mk.op("act", lambda e: e.copy(out=win[p][ci][:, 4:516], in_=env.pb[bi][:]), xreads=[env.bpb[bi]], writes=[bwin[p][ci]])
                if T > 0:
                    mk.op("act", lambda e: e.copy(out=win[p][ci][:, 0:4], in_=win[1 - p][ci][:, 512:516]), reads=[bwin[1 - p][ci]], writes=[bwin[p][ci]])
                if T == 0:
                    epilogue(ci, win[p][ci], bwin[p][ci], 512, 2, 0)
                else:
                    epilogue(ci, win[p][ci], bwin[p][ci], 512, 0, T * 512 - 2)
            if small is not None:
                for blk in range(4):
                    for k in range(KC):
                        mk.op("pe", lambda e: e.matmul(env.pb[4][:, blk * nsm:(blk + 1) * nsm], lhsT=ht[p][:, k, blk * 128:(blk + 1) * 128], rhs=Wsm[:, k, :],
                                                       start=(k == 0), stop=(k == KC - 1)),
                              reads=[bW, bht[p]], writes=[env.bpb[4]])
                mk.op("act", lambda e: e.copy(out=small_raw[:, T * 4:(T + 1) * 4, :], in_=env.pb[4][:, 0:4 * nsm].rearrange("p (b n) -> p b n", n=nsm)),
                      xreads=[env.bpb[4]], writes=[bsmall])
        p = (NTILE - 1) % 2
        for ci in range(nch):
            mk.op("act", lambda e: e.copy(out=tail[:, ci, 0:4], in_=win[p][ci][:, 512:516]), reads=[bwin[p][ci]], writes=[btail])
            epilogue(ci, tail[:, ci, :], btail, 2, 0, SEQ - 2)
        mk.barrier()


def softplus(env, x_ap, bx, n_shape_tmp=None):
    mk = env.mk
    mk.op("act", lambda e: e.activation(out=x_ap, in_=x_ap, func=AF.Exp), reads=[], writes=[bx])
    mk.op("act", lambda e: e.activation(out=x_ap, in_=x_ap, func=AF.Ln, bias=env.onesf[:, 0:1], scale=1.0), reads=[env.bconst], writes=[bx])


def bc(ap, shape, axis):
    return ap.unsqueeze(axis).to_broadcast(shape)


def ssd_scan(env, pc, pcB, small_raw, bsmall, sp, bsp, ymb, bymb):
    mk, nc = env.mk, env.nc
    with ExitStack() as es:
        f32t = lambda n, sh: env.sb(es, n, sh, F32)
        dt = f32t("dt", [128, NBLK, 8]); A = f32t("A", [128, NBLK, 8]); cs = f32t("cs", [128, NBLK, 8]); tot = f32t("tot", [128, NBLK, 8])
        ecs = f32t("ecs", [128, NBLK, 8]); cdec = f32t("cdec", [128, NBLK, 8]); dte = f32t("dte", [128, NBLK, 8])
        aneg = f32t("aneg", [128, 8])
        bdt, bA, bcs, btot, becs, bcdec, bdte, baneg = [mk.buf() for _ in range(8)]
        mk.op("dve", lambda e: e.tensor_tensor(out=dt[:], in0=small_raw[:, :, 0:8], in1=bc(sp[:, 0:8], [128, NBLK, 8], 1), op=ALU.add),
              reads=[bsmall, bsp], writes=[bdt])
        softplus(env, dt[:], bdt)
        mk.op("act", lambda e: e.activation(out=aneg[:], in_=sp[:, 8:16], func=AF.Exp), reads=[bsp], writes=[baneg])
        mk.op("dve", lambda e: e.scalar_tensor_tensor(out=A[:], in0=dt[:], scalar=-1.0, in1=bc(aneg[:], [128, NBLK, 8], 1), op0=ALU.mult, op1=ALU.mult),
              reads=[bdt, baneg], writes=[bA])
        for c in range(NBLK):
            mk.op("pe", lambda e: e.matmul(env.pb[7][:, c * 8:c * 8 + 4], lhsT=env.cF(C_U), rhs=A[:, c, 0:4], start=True, stop=True),
                  reads=[bA, env.bconst], writes=[env.bpb[7]])
            mk.op("pe", lambda e: e.matmul(env.pb[7][:, c * 8 + 4:c * 8 + 8], lhsT=env.cF(C_UT), rhs=A[:, c, 4:8], start=True, stop=True),
                  reads=[bA, env.bconst], writes=[env.bpb[7]])
            mk.op("pe", lambda e: e.matmul(env.pb[6][:, c * 8:c * 8 + 8], lhsT=env.onesf[:], rhs=A[:, c, :], start=True, stop=True),
                  reads=[bA, env.bconst], writes=[env.bpb[6]])
        mk.op("act", lambda e: e.copy(out=cs[:].rearrange("p b n -> p (b n)"), in_=env.pb[7][:]), xreads=[env.bpb[7]], writes=[bcs])
        mk.op("act", lambda e: e.copy(out=tot[:].rearrange("p b n -> p (b n)"), in_=env.pb[6][:]), xreads=[env.bpb[6]], writes=[btot])
        mk.op("act", lambda e: e.activation(out=ecs[:], in_=cs[:], func=AF.Exp), reads=[bcs], writes=[becs])
        mk.op("act", lambda e: e.activation(out=cdec[:], in_=tot[:], func=AF.Exp), reads=[btot], writes=[bcdec])
        mk.op("dve", lambda e: e.tensor_tensor(out=dte[:], in0=tot[:], in1=cs[:], op=ALU.subtract), reads=[btot, bcs], writes=[bdte])
        mk.op("act", lambda e: e.activation(out=dte[:], in_=dte[:], func=AF.Exp), reads=[], writes=[bdte])
        dI = env.sb(es, "dI", [128, 4, 128], BF16)
        bdI = mk.buf()
        for h in range(4):
            mk.op("dve", lambda e: e.tensor_scalar(out=dI[:, h, :], in0=env.cF(C_ID), scalar1=sp[:, 16 + h:17 + h], scalar2=None, op0=ALU.mult),
                  reads=[bsp, env.bconst], writes=[bdI])
        yacc = f32t("yacc", [128, NBLK, 256])
        byacc = [mk.buf() for _ in range(NBLK)]
        hstF = [f32t(f"hstF{d}", [128, 256]) for d in range(2)]
        hstB = [env.sb(es, f"hstB{d}", [128, 256], BF16) for d in range(2)]
        bhF = [mk.buf() for _ in range(2)]
        bhB = [mk.buf() for _ in range(2)]
        for d in range(2):
            mk.op("dve", lambda e: e.memset(hstF[d][:], 0.0), writes=[bhF[d]])
            mk.op("dve", lambda e: e.memset(hstB[d][:], 0.0), writes=[bhB[d]])
        NB2 = 2
        tm = [env.sb(es, f"tm{i}", [128, 384], BF16) for i in range(NB2)]
        Gm = [f32t(f"Gm{i}", [128, 128]) for i in range(NB2)]
        aL = [f32t(f"aL{i}", [128, 4, 128]) for i in range(NB2)]
        E = [f32t(f"E{i}", [128, 4, 128]) for i in range(NB2)]
        WT = [env.sb(es, f"WT{i}", [128, 4, 128], BF16) for i in range(NB2)]
        xdt = [env.sb(es, f"xdt{i}", [128, 4, 64], BF16) for i in range(NB2)]
        xdte = [env.sb(es, f"xdte{i}", [128, 4, 64], BF16) for i in range(NB2)]
        yo = [f32t(f"yo{i}", [128, 4, 64]) for i in range(NB2)]
        t2 = [f32t(f"t2{i}", [128, 256]) for i in range(NB2)]
        btm, bGm, baL, bE, bWT, bxdt, bxdte, byo, bt2 = [[mk.buf() for _ in range(NB2)] for _ in range(9)]
        touched = [False] * NBLK
        it = 0
        for step in range(NBLK):
            for d in range(2):
                c = step if d == 0 else NBLK - 1 - step
                cols = slice(c * 128, (c + 1) * 128)
                q = it % NB2
                it += 1
                tps = env.pb[0][:].bitcast(BF16)
                mk.op("pe", lambda e: e.transpose(out=tps[:, 0:128], in_=pc[2][:, cols], identity=env.cB(C_ID)), reads=[pcB[2], env.bconst], writes=[env.bpb[0]])
                mk.op("pe", lambda e: e.transpose(out=tps[:, 128:256], in_=pc[0][:, cols], identity=env.cB(C_ID)), reads=[pcB[0], env.bconst], writes=[env.bpb[0]])
                mk.op("pe", lambda e: e.transpose(out=tps[:, 256:384], in_=pc[1][:, cols], identity=env.cB(C_ID)), reads=[pcB[1], env.bconst], writes=[env.bpb[0]])
                mk.op("act", lambda e: e.copy(out=tm[q][:], in_=tps[:, 0:384]), xreads=[env.bpb[0]], writes=[btm[q]])
                mk.op("pe", lambda e: e.matmul(env.pb[1][:, 0:128], lhsT=pc[2][:, cols], rhs=pc[3][:, cols], start=True, stop=True),
                      reads=[pcB[2], pcB[3]], writes=[env.bpb[1]])
                mk.op("dve", lambda e: e.tensor_tensor(out=Gm[q][:], in0=env.pb[1][:, 0:128], in1=env.cF(C_U if d == 0 else C_UT), op=ALU.mult),
                      reads=[env.bconst], xreads=[env.bpb[1]], writes=[bGm[q]])
                mk.op("dve", lambda e: e.tensor_tensor(out=aL[q][:], in0=bc(env.cF(C_L if d == 0 else C_LT), [128, 4, 128], 1),
                                                       in1=bc(A[:, c, d * 4:d * 4 + 4], [128, 4, 128], 2), op=ALU.mult),
                      reads=[bA, env.bconst], writes=[baL[q]])
                for h in range(4):
                    mk.op("pe", lambda e: e.matmul(env.pb[2][:, h * 128:(h + 1) * 128], lhsT=aL[q][:, h, :], rhs=env.cF(C_U if d == 0 else C_UT), start=True, stop=True),
                          reads=[baL[q], env.bconst], writes=[env.bpb[2]])
                mk.op("act", lambda e: e.activation(out=E[q][:].rearrange("p h l -> p (h l)"), in_=env.pb[2][:], func=AF.Exp), xreads=[env.bpb[2]], writes=[bE[q]])
                mk.op("dve", lambda e: e.tensor_tensor(out=WT[q][:], in0=E[q][:], in1=bc(Gm[q][:], [128, 4, 128], 1), op=ALU.mult),
                      reads=[bE[q], bGm[q]], writes=[bWT[q]])
                xtm = tm[q][:, 128:384].rearrange("p (h e) -> p h e", e=64)
                mk.op("dve", lambda e: e.tensor_tensor(out=xdt[q][:], in0=xtm, in1=bc(dt[:, c, d * 4:d * 4 + 4], [128, 4, 64], 2), op=ALU.mult),
                      reads=[btm[q], bdt], writes=[bxdt[q]])
                mk.op("dve", lambda e: e.tensor_tensor(out=xdte[q][:], in0=xdt[q][:], in1=bc(dte[:, c, d * 4:d * 4 + 4], [128, 4, 64], 2), op=ALU.mult),
                      reads=[bxdt[q], bdte], writes=[bxdte[q]])
                for h in range(4):
                    mk.op("pe", lambda e: e.matmul(env.pb[3][:, h * 64:(h + 1) * 64], lhsT=WT[q][:, h, :], rhs=xdt[q][:, h, :], start=True, stop=(d == 1)),
                          reads=[bWT[q], bxdt[q]], writes=[env.bpb[3]])
                    if d == 0:
                        mk.op("pe", lambda e: e.matmul(env.pb[3][:, h * 64:(h + 1) * 64], lhsT=dI[:, h, :], rhs=tm[q][:, 128 + h * 64:128 + (h + 1) * 64],
                                                       start=False, stop=True),
                              reads=[bdI, btm[q]], writes=[env.bpb[3]])
                mk.op("pe", lambda e: e.matmul(env.pb[4][:, 0:256], lhsT=pc[3][:, cols], rhs=hstB[d][:], start=True, stop=True),
                      reads=[pcB[3], bhB[d]], writes=[env.bpb[4]])
                for h in range(4):
                    mk.op("act", lambda e: e.activation(out=yo[q][:, h, :], in_=env.pb[4][:, h * 64:(h + 1) * 64], func=AF.Copy, scale=ecs[:, c, d * 4 + h:d * 4 + h + 1]),
                          reads=[becs], xreads=[env.bpb[4]], writes=[byo[q]])
                if not touched[c]:
                    touched[c] = True
                    mk.op("dve", lambda e: e.tensor_tensor(out=yacc[:, c, :], in0=env.pb[3][:, 0:256], in1=yo[q][:].rearrange("p h e -> p (h e)"), op=ALU.add),
                          reads=[byo[q]], xreads=[env.bpb[3]], writes=[byacc[c]])
                else:
                    mk.op("dve", lambda e: e.tensor_tensor(out=t2[q][:], in0=env.pb[3][:, 0:256], in1=yo[q][:].rearrange("p h e -> p (h e)"), op=ALU.add),
                          reads=[byo[q]], xreads=[env.bpb[3]], writes=[bt2[q]])
                    mk.op("dve", lambda e: e.tensor_tensor(out=yacc[:, c, :], in0=yacc[:, c, :], in1=t2[q][:], op=ALU.add),
                          reads=[bt2[q]], writes=[byacc[c]])
                mk.op("pe", lambda e: e.matmul(env.pb[5][:, 0:256], lhsT=tm[q][:, 0:128], rhs=xdte[q][:].rearrange("p h e -> p (h e)"), start=True, stop=True),
                      reads=[btm[q], bxdte[q]], writes=[env.bpb[5]])
                mk.op("dve", lambda e: e.tensor_tensor(out=hstF[d][:].rearrange("p (h e) -> p h e", e=64), in0=hstF[d][:].rearrange("p (h e) -> p h e", e=64),
                                                       in1=bc(cdec[:, c, d * 4:d * 4 + 4], [128, 4, 64], 2), op=ALU.mult),
                      reads=[bcdec], writes=[bhF[d]])
                mk.op("dve", lambda e: e.tensor_tensor(out=hstF[d][:], in0=hstF[d][:], in1=env.pb[5][:, 0:256], op=ALU.add),
                      xreads=[env.bpb[5]], writes=[bhF[d]])
                mk.op("act", lambda e: e.copy(out=hstB[d][:], in_=hstF[d][:]), reads=[bhF[d]], writes=[bhB[d]])
        stg = [env.sb(es, f"stg{i}", [128, 512], BF16) for i in range(2)]
        bstg = [mk.buf() for _ in range(2)]
        it = 0
        for g4 in range(NBLK // 4):
            for half in range(2):
                q = it % 2
                bi = 6 + (it % 2)
                it += 1
                for j in range(4):
                    c = g4 * 4 + j
                    mk.op("pe", lambda e: e.transpose(out=env.pb[bi][:, j * 128:(j + 1) * 128], in_=yacc[:, c, half * 128:(half + 1) * 128], identity=env.cF(C_ID)),
                          reads=[byacc[c], env.bconst], writes=[env.bpb[bi]])
                mk.op("act", lambda e: e.copy(out=stg[q][:], in_=env.pb[bi][:]), xreads=[env.bpb[bi]], writes=[bstg[q]])
                mk.dma("sp", ymb[half, :, g4 * 512:(g4 + 1) * 512], stg[q][:], "ymb", reads=[bstg[q]], writes=[bymb])
        mk.barrier()


def phase_mb(env, es, hTall, wmb, cwd, spd, ymb, do_gdn=True):
    mk = env.mk
    cw = env.sb(es, "cw", [128, 10, 6], F32)
    sp = env.sb(es, "sp", [128, 32], F32)
    bcw, bsp, bymb = mk.buf(), mk.buf(), mk.buf()
    mk.dma("sp", cw[:], cwd, "cw", writes=[bcw])
    mk.dma("sp", sp[:], spd, "sp", writes=[bsp])
    small_raw = env.sb(es, "small_raw", [128, NBLK, 16], F32)
    bsmall = mk.buf()
    with ExitStack() as es2:
        pc = [env.sb(es2, f"pcS{i}", [128, SEQ], BF16) for i in range(4)]
        pcB = [mk.buf() for _ in range(4)]
        inproj_conv_pass(env, hTall, wmb, 0, 4, cw, bcw, [0, 1, 2, 3], pc, pcB, [None] * 4, small=(1280, 16, small_raw, bsmall))
        ssd_scan(env, pc, pcB, small_raw, bsmall, sp, bsp, ymb, bymb)
    if do_gdn:
        with ExitStack() as es3:
            pc = [env.sb(es3, f"pcG{i}", [128, SEQ], BF16) for i in range(6)]
            pcB = [mk.buf() for _ in range(6)]
            inproj_conv_pass(env, hTall, wmb, 512, 6, cw, bcw, [4, 5, 6, 7, 8, 9], pc, pcB, [QSCALE, 1.0, None, QSCALE, 1.0, None])
            gdn_scan(env, pc, pcB, small_raw, bsmall, sp, bsp, ymb, bymb)


def gdn_scan(env, pc, pcB, small_raw, bsmall, sp, bsp, ymb, bymb):
    mk, nc = env.mk, env.nc
    with ExitStack() as es:
        f32t = lambda n, sh: env.sb(es, n, sh, F32)
        bft = lambda n, sh: env.sb(es, n, sh, BF16)
        g = f32t("g", [128, NBLK, 4]); beta = f32t("beta", [128, NBLK, 4]); gcs = f32t("gcs", [128, NBLK, 4]); tot = f32t("tot64", [128, NBLK, 4])
        eg = f32t("eg", [128, NBLK, 4]); nbeta = f32t("nbeta", [128, NBLK, 4]); beg = f32t("beg", [128, NBLK, 4]); kes = f32t("kes", [128, NBLK, 4])
        ld = f32t("ld", [128, NBLK, 2, 4]); an = f32t("an", [128, 4])
        bg_, bbeta, bgcs, btot, beg_, bnbeta, bbeg, bkes, bld, ban = [mk.buf() for _ in range(10)]
        mk.op("dve", lambda e: e.tensor_tensor(out=g[:], in0=small_raw[:, :, 8:12], in1=bc(sp[:, 20:24], [128, NBLK, 4], 1), op=ALU.add),
              reads=[bsmall, bsp], writes=[bg_])
        softplus(env, g[:], bg_)
        mk.op("act", lambda e: e.activation(out=an[:], in_=sp[:, 24:28], func=AF.Exp), reads=[bsp], writes=[ban])
        mk.op("dve", lambda e: e.scalar_tensor_tensor(out=g[:], in0=g[:], scalar=-1.0, in1=bc(an[:], [128, NBLK, 4], 1), op0=ALU.mult, op1=ALU.mult),
              reads=[ban], writes=[bg_])
        mk.op("act", lambda e: e.activation(out=beta[:], in_=small_raw[:, :, 12:16], func=AF.Sigmoid), reads=[bsmall], writes=[bbeta])
        for b in range(NBLK):
            for d in range(2):
                mk.op("pe", lambda e: e.matmul(env.pb[7][:, b * 4 + d * 2:b * 4 + d * 2 + 2], lhsT=env.cF(C_U64 if d == 0 else C_UT64), rhs=g[:, b, d * 2:d * 2 + 2],
                                               start=True, stop=True), reads=[bg_, env.bconst], writes=[env.bpb[7]])
            mk.op("pe", lambda e: e.matmul(env.pb[7][:, 256 + b * 4:256 + b * 4 + 4], lhsT=env.cF(C_BD), rhs=g[:, b, :], start=True, stop=True),
                  reads=[bg_, env.bconst], writes=[env.bpb[7]])
            for hf in range(2):
                mk.op("pe", lambda e: e.matmul(env.pb[6][:, b * 8 + hf * 4:b * 8 + hf * 4 + 4], lhsT=env.cF(C_H0 if hf == 0 else C_H1), rhs=g[:, b, :], start=True, stop=True),
                      reads=[bg_, env.bconst], writes=[env.bpb[6]])
        mk.op("act", lambda e: e.copy(out=gcs[:].rearrange("p b n -> p (b n)"), in_=env.pb[7][:, 0:256]), xreads=[env.bpb[7]], writes=[bgcs])
        mk.op("act", lambda e: e.copy(out=tot[:].rearrange("p b n -> p (b n)"), in_=env.pb[7][:, 256:512]), xreads=[env.bpb[7]], writes=[btot])
        mk.op("act", lambda e: e.activation(out=ld[:].rearrange("p b h n -> p (b h n)"), in_=env.pb[6][:], func=AF.Exp), xreads=[env.bpb[6]], writes=[bld])
        mk.op("act", lambda e: e.activation(out=eg[:], in_=gcs[:], func=AF.Exp), reads=[bgcs], writes=[beg_])
        mk.op("dve", lambda e: e.tensor_scalar(out=nbeta[:], in0=beta[:], scalar1=-1.0, scalar2=None, op0=ALU.mult), reads=[bbeta], writes=[bnbeta])
        mk.op("dve", lambda e: e.tensor_tensor(out=beg[:], in0=beta[:], in1=eg[:], op=ALU.mult), reads=[bbeta, beg_], writes=[bbeg])
        mk.op("dve", lambda e: e.tensor_tensor(out=kes[:], in0=tot[:], in1=gcs[:], op=ALU.subtract), reads=[btot, bgcs], writes=[bkes])
        mk.op("act", lambda e: e.activation(out=kes[:], in_=kes[:], func=AF.Exp), reads=[], writes=[bkes])
        maskp = [f32t(f"maskp{d}", [128, 4, 128]) for d in range(2)]
        bmask = mk.buf()
        for d in range(2):
            for i4, ci in enumerate([C_L64, C_L64, C_U64, C_U64] if d == 0 else [C_LT64, C_LT64, C_UT64, C_UT64]):
                mk.op("dve", lambda e: e.tensor_copy(out=maskp[d][:, i4, :], in_=env.cF(ci)), reads=[env.bconst], writes=[bmask])
        oacc = bft("oacc", [128, NBLK, 256])
        boacc = [[mk.buf() for _ in range(2)] for _ in range(NBLK)]
        Sf = [f32t(f"Sf{d}", [128, 2, 128]) for d in range(2)]
        Sb = [bft(f"Sb{d}", [128, 2, 128]) for d in range(2)]
        bSf = [mk.buf() for _ in range(2)]
        bSb = [mk.buf() for _ in range(2)]
        for d in range(2):
            mk.op("dve", lambda e: e.memset(Sf[d][:], 0.0), writes=[bSf[d]])
            mk.op("dve", lambda e: e.memset(Sb[d][:], 0.0), writes=[bSb[d]])
        NB2 = 2
        mkl = lambda fn: [fn(i) for i in range(NB2)]
        tmk = mkl(lambda i: bft(f"tmk{i}", [128, 4, 128]))
        kbg = mkl(lambda i: f32t(f"kbg{i}", [128, 2, 128])); bv = mkl(lambda i: f32t(f"bv{i}", [128, 2, 128])); ke = mkl(lambda i: bft(f"ke{i}", [128, 2, 128]))
        gUL = mkl(lambda i: f32t(f"gUL{i}", [128, 4, 128])); Eall = mkl(lambda i: f32t(f"Eall{i}", [128, 4, 128]))
        PP = [[f32t(f"PP{i}_{k}", [128, 4, 128]) for k in range(2)] for i in range(NB2)]
        RR = [[f32t(f"RR{i}_{k}", [128, 2, 128]) for k in range(2)] for i in range(NB2)]
        aqT = mkl(lambda i: bft(f"aqT{i}", [128, 2, 128])); u = mkl(lambda i: f32t(f"u{i}", [128, 2, 128])); wT = mkl(lambda i: bft(f"wT{i}", [128, 2, 128]))
        vnew = mkl(lambda i: bft(f"vnew{i}", [128, 2, 128])); tt = mkl(lambda i: f32t(f"tt{i}", [128, 2, 128])); t2 = mkl(lambda i: f32t(f"t2_{i}", [128, 2, 128]))
        (btmk, bkbg, bbv, bke, bgUL, bEall, baqT, bu, bwT, bvnew, btt, bt2) = [[mk.buf() for _ in range(NB2)] for _ in range(12)]
        bPP = [[mk.buf() for _ in range(2)] for _ in range(NB2)]
        bRR = [[mk.buf() for _ in range(2)] for _ in range(NB2)]
        touched = [[False, False] for _ in range(NBLK)]
        it = 0
        for step in range(NBLK):
            for d in range(2):
                b = step if d == 0 else NBLK - 1 - step
                cols = slice(b * 128, (b + 1) * 128)
                q = it % NB2
                it += 1
                d2 = slice(d * 2, d * 2 + 2)
                qT = [pc[0][:, cols], pc[3][:, cols]]; kT = [pc[1][:, cols], pc[4][:, cols]]; vT = [pc[2][:, cols], pc[5][:, cols]]
                bq = [pcB[0], pcB[3]]; bk = [pcB[1], pcB[4]]; bvv = [pcB[2], pcB[5]]
                tps = env.pb[0][:].bitcast(BF16)
                for j in range(2):
                    mk.op("pe", lambda e: e.transpose(out=tps[:, j * 128:(j + 1) * 128], in_=kT[j], identity=env.cB(C_ID)), reads=[bk[j], env.bconst], writes=[env.bpb[0]])
                    mk.op("pe", lambda e: e.transpose(out=tps[:, 256 + j * 128:256 + (j + 1) * 128], in_=vT[j], identity=env.cB(C_ID)), reads=[bvv[j], env.bconst], writes=[env.bpb[0]])
                mk.op("act", lambda e: e.copy(out=tmk[q][:].rearrange("p a l -> p (a l)"), in_=tps[:, 0:512]), xreads=[env.bpb[0]], writes=[btmk[q]])
                mk.op("dve", lambda e: e.tensor_tensor(out=kbg[q][:], in0=tmk[q][:, 0:2, :], in1=bc(beg[:, b, d2], [128, 2, 128], 2), op=ALU.mult), reads=[btmk[q], bbeg], writes=[bkbg[q]])
                mk.op("dve", lambda e: e.tensor_tensor(out=bv[q][:], in0=tmk[q][:, 2:4, :], in1=bc(beta[:, b, d2], [128, 2, 128], 2), op=ALU.mult), reads=[btmk[q], bbeta], writes=[bbv[q]])
                mk.op("dve", lambda e: e.tensor_tensor(out=ke[q][:], in0=tmk[q][:, 0:2, :], in1=bc(kes[:, b, d2], [128, 2, 128], 2), op=ALU.mult), reads=[btmk[q], bkes], writes=[bke[q]])
                for j in range(2):
                    mk.op("pe", lambda e: e.matmul(env.pb[1][:, j * 128:(j + 1) * 128], lhsT=kT[j], rhs=kT[j], start=True, stop=True), reads=[bk[j]], writes=[env.bpb[1]])
                    mk.op("pe", lambda e: e.matmul(env.pb[1][:, 256 + j * 128:256 + (j + 1) * 128], lhsT=kT[j], rhs=qT[j], start=True, stop=True), reads=[bk[j], bq[j]], writes=[env.bpb[1]])
                cU, cL = (C_U64, C_L64) if d == 0 else (C_UT64, C_LT64)
                mk.op("dve", lambda e: e.tensor_tensor(out=gUL[q][:, 0:2, :], in0=bc(env.cF(cU), [128, 2, 128], 1), in1=bc(g[:, b, d2], [128, 2, 128], 2), op=ALU.mult),
                      reads=[bg_, env.bconst], writes=[bgUL[q]])
                mk.op("dve", lambda e: e.tensor_tensor(out=gUL[q][:, 2:4, :], in0=bc(env.cF(cL), [128, 2, 128], 1), in1=bc(g[:, b, d2], [128, 2, 128], 2), op=ALU.mult),
                      reads=[bg_, env.bconst], writes=[bgUL[q]])
                for j in range(2):
                    mk.op("pe", lambda e: e.matmul(env.pb[2][:, j * 128:(j + 1) * 128], lhsT=gUL[q][:, j, :], rhs=env.cF(cL), start=True, stop=True),
                          reads=[bgUL[q], env.bconst], writes=[env.bpb[2]])
                    mk.op("pe", lambda e: e.matmul(env.pb[2][:, 256 + j * 128:256 + (j + 1) * 128], lhsT=gUL[q][:, 2 + j, :], rhs=env.cF(cU), start=True, stop=True),
                          reads=[bgUL[q], env.bconst], writes=[env.bpb[2]])
                mk.op("act", lambda e: e.activation(out=Eall[q][:].rearrange("p a l -> p (a l)"), in_=env.pb[2][:], func=AF.Exp), xreads=[env.bpb[2]], writes=[bEall[q]])
                mk.op("dve", lambda e: e.tensor_tensor(out=Eall[q][:], in0=Eall[q][:], in1=maskp[d][:], op=ALU.mult), reads=[bmask], writes=[bEall[q]])
                for j in range(2):
                    mk.op("dve", lambda e: e.scalar_tensor_tensor(out=PP[q][0][:, 2 + j, :], in0=env.pb[1][:, j * 128:(j + 1) * 128], scalar=nbeta[:, b, d * 2 + j:d * 2 + j + 1],
                                                                  in1=Eall[q][:, j, :], op0=ALU.mult, op1=ALU.mult),
                          reads=[bnbeta, bEall[q]], xreads=[env.bpb[1]], writes=[bPP[q][0]])
                mk.op("dve", lambda e: e.tensor_tensor(out=aqT[q][:], in0=env.pb[1][:, 256:512].rearrange("p (a l) -> p a l", l=128), in1=Eall[q][:, 2:4, :], op=ALU.mult),
                      reads=[bEall[q]], xreads=[env.bpb[1]], writes=[baqT[q]])
                for j in range(2):
                    mk.op("pe", lambda e: e.transpose(out=env.pb[3][:, j * 128:(j + 1) * 128], in_=PP[q][0][:, 2 + j, :], identity=env.cF(C_ID)),
                          reads=[bPP[q][0], env.bconst], writes=[env.bpb[3]])
                mk.op("act", lambda e: e.copy(out=PP[q][0][:, 0:2, :].rearrange("p a l -> p (a l)"), in_=env.pb[3][:, 0:256]), xreads=[env.bpb[3]], writes=[bPP[q][0]])
                mk.op("dve", lambda e: e.tensor_tensor(out=RR[q][0][:], in0=PP[q][0][:, 0:2, :], in1=bc(env.cF(C_ID), [128, 2, 128], 1), op=ALU.add),
                      reads=[bPP[q][0], env.bconst], writes=[bRR[q][0]])
                for k in range(1, 6):
                    src, dst = PP[q][(k - 1) % 2], PP[q][k % 2]
                    bsrc, bdst = bPP[q][(k - 1) % 2], bPP[q][k % 2]
                    for j in range(2):
                        if k < 5:
                            mk.op("pe", lambda e: e.matmul(env.pb[3][:, j * 128:(j + 1) * 128], lhsT=src[:, 2 + j, :], rhs=src[:, j, :], start=True, stop=True),
                                  reads=[bsrc], writes=[env.bpb[3]])
                        mk.op("pe", lambda e: e.matmul(env.pb[3][:, 256 + j * 128:256 + (j + 1) * 128], lhsT=src[:, j, :], rhs=src[:, 2 + j, :], start=True, stop=True),
                              reads=[bsrc], writes=[env.bpb[3]])
                    lo = 0 if k < 5 else 256
                    mk.op("act", lambda e: e.copy(out=dst[:].rearrange("p a l -> p (a l)")[:, lo:512], in_=env.pb[3][:, lo:512]), xreads=[env.bpb[3]], writes=[bdst])
                    rs_, rd_ = RR[q][(k - 1) % 2], RR[q][k % 2]
                    brs_, brd_ = bRR[q][(k - 1) % 2], bRR[q][k % 2]
                    for j in range(2):
                        mk.op("pe", lambda e: e.matmul(env.pb[4][:, j * 128:(j + 1) * 128], lhsT=dst[:, 2 + j, :], rhs=rs_[:, j, :], start=True, stop=True),
                              reads=[bdst, brs_], writes=[env.bpb[4]])
                    mk.op("dve", lambda e: e.tensor_tensor(out=rd_[:], in0=rs_[:], in1=env.pb[4][:, 0:256].rearrange("p (a l) -> p a l", l=128), op=ALU.add),
                          reads=[brs_], xreads=[env.bpb[4]], writes=[brd_])
                TT, bTT = RR[q][1], bRR[q][1]
                for j in range(2):
                    mk.op("pe", lambda e: e.matmul(env.pb[5][:, j * 128:(j + 1) * 128], lhsT=TT[:, j, :], rhs=bv[q][:, j, :], start=True, stop=True),
                          reads=[bTT, bbv[q]], writes=[env.bpb[5]])
                    mk.op("pe", lambda e: e.matmul(env.pb[5][:, 256 + j * 128:256 + (j + 1) * 128], lhsT=kbg[q][:, j, :], rhs=TT[:, j, :], start=True, stop=True),
                          reads=[bTT, bkbg[q]], writes=[env.bpb[5]])
                mk.op("act", lambda e: e.copy(out=u[q][:].rearrange("p a l -> p (a l)"), in_=env.pb[5][:, 0:256]), xreads=[env.bpb[5]], writes=[bu[q]])
                mk.op("act", lambda e: e.copy(out=wT[q][:].rearrange("p a l -> p (a l)"), in_=env.pb[5][:, 256:512]), xreads=[env.bpb[5]], writes=[bwT[q]])
                cb = 6 + d
                ob = 0 if d == 0 else 1
                for hf in ([0, 1] if d == 0 else [1, 0]):
                    rows = slice(hf * 64, hf * 64 + 64)
                    for j in range(2):
                        mk.op("pe", lambda e: e.matmul(env.pb[cb][:, j * 128:(j + 1) * 128], lhsT=wT[q][:, j, :], rhs=Sb[d][:, j, :], start=True, stop=True),
                              reads=[bwT[q], bSb[d]], writes=[env.bpb[cb]])
                    mk.op("dve", lambda e: e.tensor_tensor(out=vnew[q][rows, :, :], in0=u[q][rows, :, :], in1=env.pb[cb][rows, 0:256].rearrange("p (a l) -> p a l", l=128), op=ALU.subtract),
                          reads=[bu[q]], xreads=[env.bpb[cb]], writes=[bvnew[q]])
                    for j in range(2):
                        mk.op("pe", lambda e: e.matmul(env.pb[ob][:, j * 128:(j + 1) * 128], lhsT=qT[j], rhs=Sb[d][:, j, :], start=True, stop=True),
                              reads=[bq[j], bSb[d]], writes=[env.bpb[ob]])
                        mk.op("pe", lambda e: e.matmul(env.pb[ob][:, 256 + j * 128:256 + (j + 1) * 128], lhsT=aqT[q][rows, j, :], rhs=vnew[q][rows, j, :], start=True, stop=True),
                              reads=[baqT[q], bvnew[q]], writes=[env.bpb[ob]])
                        mk.op("pe", lambda e: e.matmul(env.pb[cb][:, 256 + j * 128:256 + (j + 1) * 128], lhsT=ke[q][rows, j, :], rhs=vnew[q][rows, j, :], start=True, stop=True),
                              reads=[bke[q], bvnew[q]], writes=[env.bpb[cb]])
                    for j in range(2):
                        mk.op("dve", lambda e: e.scalar_tensor_tensor(out=Sf[d][:, j, :], in0=Sf[d][:, j, :], scalar=ld[:, b, hf, d * 2 + j:d * 2 + j + 1],
                                                                      in1=env.pb[cb][:, 256 + j * 128:256 + (j + 1) * 128], op0=ALU.mult, op1=ALU.add),
                              reads=[bld], xreads=[env.bpb[cb]], writes=[bSf[d]])
                    mk.op("act", lambda e: e.copy(out=Sb[d][:], in_=Sf[d][:]), reads=[bSf[d]], writes=[bSb[d]])
                    for j in range(2):
                        mk.op("act", lambda e: e.activation(out=tt[q][rows, j, :], in_=env.pb[ob][rows, j * 128:(j + 1) * 128], func=AF.Copy, scale=eg[rows, b, d * 2 + j:d * 2 + j + 1]),
                              reads=[beg_], xreads=[env.bpb[ob]], writes=[btt[q]])
                    o2 = env.pb[ob][rows, 256:512].rearrange("p (a l) -> p a l", l=128)
                    oa = oacc[rows, b, :].rearrange("p (a l) -> p a l", l=128)
                    if not touched[b][hf]:
                        touched[b][hf] = True
                        mk.op("dve", lambda e: e.tensor_tensor(out=oa, in0=o2, in1=tt[q][rows, :, :], op=ALU.add), reads=[btt[q]], xreads=[env.bpb[ob]], writes=[boacc[b][hf]])
                    else:
                        mk.op("dve", lambda e: e.tensor_tensor(out=t2[q][rows, :, :], in0=o2, in1=tt[q][rows, :, :], op=ALU.add), reads=[btt[q]], xreads=[env.bpb[ob]], writes=[bt2[q]])
                        mk.op("dve", lambda e: e.tensor_tensor(out=oa, in0=oa, in1=t2[q][rows, :, :], op=ALU.add), reads=[bt2[q]], writes=[boacc[b][hf]])
        stg = [bft(f"gstg{i}", [128, 512]) for i in range(2)]
        bstg = [mk.buf() for _ in range(2)]
        it = 0
        for g4 in range(NBLK // 4):
            for j in range(2):
                q = it % 2
                bi = 2 + (it % 2)
                it += 1
                tps = env.pb[bi][:].bitcast(BF16)
                for jj in range(4):
                    b = g4 * 4 + jj
                    mk.op("pe", lambda e: e.transpose(out=tps[:, jj * 128:(jj + 1) * 128], in_=oacc[:, b, j * 128:(j + 1) * 128], identity=env.cB(C_ID)),
                          reads=[boacc[b][0], boacc[b][1], env.bconst], writes=[env.bpb[bi]])
                mk.op("act", lambda e: e.copy(out=stg[q][:], in_=tps[:, 0:512]), xreads=[env.bpb[bi]], writes=[bstg[q]])
                mk.dma("sp", ymb[2 + j, :, g4 * 512:(g4 + 1) * 512], stg[q][:], "ymb", reads=[bstg[q]], writes=[bymb])
        mk.barrier()


import numpy as np
import ml_dtypes
BF = ml_dtypes.bfloat16

def mb_cols(r):
    g = r // 2
    cols = []
    cols += list(range(2048 + 256 * r, 2048 + 256 * r + 256))
    cols += list(range(4096 + 128 * g, 4096 + 128 * g + 128))
    cols += list(range(4608 + 128 * g, 4608 + 128 * g + 128))
    for j in range(2):
        hd = 2 * r + j
        cols += list(range(5184 + 128 * hd, 5184 + 128 * hd + 128))
        cols += list(range(5184 + 2048 + 128 * hd, 5184 + 2048 + 128 * hd + 128))
        cols += list(range(5184 + 4096 + 128 * hd, 5184 + 4096 + 128 * hd + 128))
    for d in range(2):
        cols += [5120 + d * 32 + 4 * r + h for h in range(4)]
    for ab in range(2):
        for d in range(2):
            cols += [13376 + ab * 32 + d * 16 + 2 * r + j for j in range(2)]
    return np.array(cols)

def mb_conv(r, ssd_w, ssd_b, gdn_w):
    g = r // 2
    cw = np.zeros((128, 10, 6), np.float32)
    def put(ci, w, b, ch0):
        cw[:, ci, 0:5] = w[:, ch0:ch0 + 128].T
        if b is not None:
            cw[:, ci, 5] = b[ch0:ch0 + 128]
    put(0, ssd_w, ssd_b, 256 * r); put(1, ssd_w, ssd_b, 256 * r + 128)
    put(2, ssd_w, ssd_b, 2048 + 128 * g); put(3, ssd_w, ssd_b, 2560 + 128 * g)
    for j in range(2):
        hd = 2 * r + j
        put(4 + 3 * j, gdn_w, None, 128 * hd); put(5 + 3 * j, gdn_w, None, 2048 + 128 * hd); put(6 + 3 * j, gdn_w, None, 4096 + 128 * hd)
    return cw

def mb_sp(r, ssd_dt_bias, ssd_a_log, ssd_d, gdn_dt_bias, gdn_a_log):
    v = np.zeros(32, np.float32)
    for d in range(2):
        for h in range(4):
            v[d * 4 + h] = ssd_dt_bias[d, 4 * r + h]; v[8 + d * 4 + h] = ssd_a_log[d, 4 * r + h]
        for j in range(2):
            v[20 + d * 2 + j] = gdn_dt_bias[d, 2 * r + j]; v[24 + d * 2 + j] = gdn_a_log[d, 2 * r + j]
    for h in range(4):
        v[16 + h] = ssd_d[4 * r + h]
    return np.ascontiguousarray(np.broadcast_to(v, (128, 32)))

def to_fm(x, kc=16):
    return np.ascontiguousarray(x.T.reshape(kc, 128, -1).transpose(1, 0, 2))

def from_fm(xT):
    return xT.transpose(1, 0, 2).reshape(xT.shape[0] * xT.shape[1], -1).T

def col(g, kc=16):
    return np.ascontiguousarray(g.reshape(kc, 128).T)


_PROGS = {}


def build_p1():
    nc = bass.Bass("TRN2", target_bir_lowering=False)
    xT = nc.dram_tensor("xT", [128, KC, NT], F32, kind="ExternalInput").ap()
    w_gu = nc.dram_tensor("w_gu", [D, 2 * DFF], F32, kind="ExternalInput").ap()
    w_down = nc.dram_tensor("w_down", [DFF, D], F32, kind="ExternalInput").ap()
    gv = nc.dram_tensor("gv", [128, 3 * KC], F32, kind="ExternalInput").ap()
    consts = nc.dram_tensor("consts", [128, NCONST * 128], F32, kind="ExternalInput").ap()
    x1T = nc.dram_tensor("x1T", [128, KC, NT], F32, kind="ExternalOutput").ap()
    hTo = nc.dram_tensor("hTo", [128, KC, NT], BF16, kind="ExternalOutput").ap()
    with ExitStack() as es:
        mk = MK(nc, es)
        env = Env(nc, mk, es, consts)
        g = env.sb(es, "gv", [128, 3 * KC], F32)
        bg = mk.buf()
        mk.dma("sp", g[:], gv, "gv", writes=[bg])
        mk.op("act", lambda e: e.mul(out=g[:, KC:2 * KC], in_=g[:, KC:2 * KC], mul=0.5), reads=[bg], writes=[bg])
        bx1, bh = mk.buf(), mk.buf()
        phase_ffn(env, xT, x1T, bx1, w_gu, w_down, g[:, 0:KC], g[:, KC:2 * KC], bg)
        mk.barrier()
        phase_ma(env, x1T, g[:, 2 * KC:3 * KC], bg, hTo, bh)
        mk.barrier()
    return nc


def build_p2():
    nc = bass.Bass("TRN2", target_bir_lowering=False)
    hTall = nc.dram_tensor("hTall", [128, KC, SEQ], BF16, kind="ExternalInput").ap()
    wmb = nc.dram_tensor("wmb", [D, 1296], F32, kind="ExternalInput").ap()
    cwd = nc.dram_tensor("cw", [128, 10, 6], F32, kind="ExternalInput").ap()
    spd = nc.dram_tensor("sp", [128, 32], F32, kind="ExternalInput").ap()
    consts = nc.dram_tensor("consts", [128, NCONST * 128], F32, kind="ExternalInput").ap()
    ymb = nc.dram_tensor("ymb", [4, 128, SEQ], BF16, kind="ExternalOutput").ap()
    with ExitStack() as es:
        mk = MK(nc, es)
        env = Env(nc, mk, es, consts)
        phase_mb(env, es, hTall, wmb, cwd, spd, ymb, do_gdn=True)
        mk.barrier()
    return nc


def build_p3():
    nc = bass.Bass("TRN2", target_bir_lowering=False)
    x1T = nc.dram_tensor("x1T", [128, KC, NT], F32, kind="ExternalInput").ap()
    hTi = nc.dram_tensor("hTi", [128, KC, NT], BF16, kind="ExternalInput").ap()
    yTi = nc.dram_tensor("yTi", [128, 32, NT], BF16, kind="ExternalInput").ap()
    w_z = nc.dram_tensor("w_z", [D, 4096], F32, kind="ExternalInput").ap()
    w_out = nc.dram_tensor("w_out", [4096, D], F32, kind="ExternalInput").ap()
    gsd = nc.dram_tensor("gs", [128, 33], F32, kind="ExternalInput").ap()
    w_gu = nc.dram_tensor("w_gu", [D, 2 * DFF], F32, kind="ExternalInput").ap()
    w_down = nc.dram_tensor("w_down", [DFF, D], F32, kind="ExternalInput").ap()
    gv = nc.dram_tensor("gv", [128, 2 * KC], F32, kind="ExternalInput").ap()
    consts = nc.dram_tensor("consts", [128, NCONST * 128], F32, kind="ExternalInput").ap()
    x2T = nc.dram_tensor("x2T", [128, KC, NT], F32).ap()
    x3T = nc.dram_tensor("x3T", [128, KC, NT], F32, kind="ExternalOutput").ap()
    with ExitStack() as es:
        mk = MK(nc, es)
        env = Env(nc, mk, es, consts)
        g = env.sb(es, "gv", [128, 2 * KC], F32)
        gs = env.sb(es, "gs", [128, 33], F32)
        bg = mk.buf()
        mk.dma("sp", g[:], gv, "gv", writes=[bg])
        mk.dma("sp", gs[:], gsd, "gv", writes=[bg])
        mk.op("act", lambda e: e.mul(out=g[:, KC:2 * KC], in_=g[:, KC:2 * KC], mul=0.5), reads=[bg], writes=[bg])
        bx2, bx3 = mk.buf(), mk.buf()
        phase_mc(env, x1T, x2T, bx2, hTi, yTi, w_z, w_out, gs, bg, zcols=[i * 128 for i in range(32)])
        mk.barrier()
        phase_ffn(env, x2T, x3T, bx3, w_gu, w_down, g[:, 0:KC], g[:, KC:2 * KC], bg)
        mk.barrier()
    return nc


def _prog(name):
    if name not in _PROGS:
        _PROGS[name] = {"p1": build_p1, "p2": build_p2, "p3": build_p3}[name]()
    return _PROGS[name]


ZCOLS = np.concatenate([np.arange(0, 2048), np.arange(11328, 11328 + 2048)])


def kernel(**inp):
    f32 = np.float32
    x = np.asarray(inp["x"], f32)[0]
    consts = make_consts()
    cores = list(range(NCORES))
    xT = [to_fm(x[r * NT:(r + 1) * NT]) for r in cores]
    for L in range(4):
        gI = lambda k: np.asarray(inp[k], f32)[L]
        gv = np.concatenate([col(gI("ffn1_pre_g")), col(gI("ffn1_post_g")), col(gI("mix_pre_g"))], 1)
        wgu, wd = np.ascontiguousarray(gI("ffn1_w_gu")), np.ascontiguousarray(gI("ffn1_w_down"))
        res = run_bass_kernel_spmd(_prog("p1"), [{"xT": xT[r], "w_gu": wgu, "w_down": wd, "gv": gv, "consts": consts} for r in cores], core_ids=cores)
        x1T = [res.results[r]["x1T"] for r in cores]
        hT = [res.results[r]["hTo"] for r in cores]
        hTall = np.ascontiguousarray(np.concatenate(hT, axis=2))
        w_in = gI("w_in")
        im = []
        for r in cores:
            im.append({"hTall": hTall, "wmb": np.ascontiguousarray(w_in[:, mb_cols(r)]),
                       "cw": mb_conv(r, gI("ssd_conv_w"), gI("ssd_conv_b"), gI("gdn_conv_w")),
                       "sp": mb_sp(r, gI("ssd_dt_bias"), gI("ssd_a_log"), gI("ssd_d"), gI("gdn_dt_bias"), gI("gdn_a_log")),
                       "consts": consts})
        res = run_bass_kernel_spmd(_prog("p2"), im, core_ids=cores)
        yall = np.empty((32, 128, SEQ), dtype=res.results[0]["ymb"].dtype)
        for r in cores:
            y = res.results[r]["ymb"]
            yall[2 * r] = y[0]; yall[2 * r + 1] = y[1]; yall[16 + 2 * r] = y[2]; yall[16 + 2 * r + 1] = y[3]
        gs = np.concatenate([col(gI("ssd_norm_g")), gI("gdn_norm_g").reshape(128, 1), col(gI("mix_post_g"))], 1)
        gv2 = np.concatenate([col(gI("ffn2_pre_g")), col(gI("ffn2_post_g"))], 1)
        w_z = np.ascontiguousarray(w_in[:, ZCOLS])
        w_out = np.ascontiguousarray(gI("w_out"))
        wgu2, wd2 = np.ascontiguousarray(gI("ffn2_w_gu")), np.ascontiguousarray(gI("ffn2_w_down"))
        im = []
        for r in cores:
            yT = np.ascontiguousarray(yall[:, :, r * NT:(r + 1) * NT].transpose(1, 0, 2))
            im.append({"x1T": x1T[r], "hTi": hT[r], "yTi": yT, "w_z": w_z, "w_out": w_out, "gs": gs,
                       "w_gu": wgu2, "w_down": wd2, "gv": gv2, "consts": consts})
        res = run_bass_kernel_spmd(_prog("p3"), im, core_ids=cores)
        xT = [res.results[r]["x3T"] for r in cores]
    out = np.concatenate([from_fm(np.asarray(xT[r], f32)) for r in cores], axis=0)
    return np.ascontiguousarray(out[None]).astype(f32)
```

```python
import numpy as np
import concourse.bass as bass
import concourse.mybir as mybir
from concourse.bass_utils import run_bass_kernel_spmd
from contextlib import ExitStack

F32 = mybir.dt.float32
BF16 = mybir.dt.bfloat16
AF = mybir.ActivationFunctionType
ALU = mybir.AluOpType

D = 2048
KC = D // 128
NT = 1024
SEQ = 8192
import os
DFF = int(os.environ.get('DFF', 5632))
FC = DFF // 128
EPS = 1e-6
NCORES = 8


class Buf:
    __slots__ = ("w", "rs", "name")

    def __init__(self, name=""):
        self.w = None
        self.rs = []
        self.name = name


class MK:
    def __init__(self, nc, es):
        self.nc = nc
        self.es = es
        self.eng = {"pe": nc.tensor, "act": nc.scalar, "dve": nc.vector, "pool": nc.gpsimd, "sp": nc.sync}
        self.sem = {k: es.enter_context(nc.semaphore("s_" + k)) for k in self.eng}
        self.cnt = {k: 0 for k in self.eng}
        self.seen = {k: {} for k in self.eng}
        self.dsems = {}
        self.nbuf = 0

    def buf(self, name=""):
        self.nbuf += 1
        return Buf(name)

    def _wait(self, e, tok):
        if tok is None:
            return
        kind, sem, val, key = tok
        if kind == "dma":
            val = self.dsems[key[2:]][1]
        if e == "pe" and key == "pe":
            return
        if self.seen[e].get(key, 0) >= val:
            return
        self.eng[e].wait_ge(sem, val)
        self.seen[e][key] = val

    def _deps(self, e, reads, writes):
        for b in reads:
            self._wait(e, b.w)
        for b in writes:
            self._wait(e, b.w)
            for t in b.rs:
                self._wait(e, t)

    def _record(self, tok, reads, writes):
        for b in reads:
            b.rs.append(tok)
            if len(b.rs) > 64:
                last = {}
                for t in b.rs:
                    last[t[3]] = t
                b.rs = list(last.values())
        for b in writes:
            b.w = tok
            b.rs = []

    def op(self, e, fn, reads=(), writes=(), xreads=()):
        writes = list(writes) + list(xreads)
        self._deps(e, reads, writes)
        ins = fn(self.eng[e])
        self.cnt[e] += 1
        ins.then_inc(self.sem[e], 1)
        tok = ("eng", self.sem[e], self.cnt[e], e)
        self._record(tok, reads, writes)
        return ins

    def dsem(self, name):
        if name not in self.dsems:
            self.dsems[name] = [self.es.enter_context(self.nc.semaphore("d_" + name)), 0]
        return self.dsems[name]

    def dma(self, q, out, in_, semname, reads=(), writes=(), **kw):
        self._deps(q, reads, writes)
        s = self.dsem(semname)
        ins = self.eng[q].dma_start(out=out, in_=in_, **kw)
        s[1] += 16
        ins.then_inc(s[0], 16)
        tok = ("dma", s[0], s[1], "d_" + semname)
        self._record(tok, reads, writes)
        return ins

    def collective(self, kind, src, dst, semname, reads=(), writes=()):
        self._deps("pool", reads, writes)
        s = self.dsem(semname)
        ins = self.nc.gpsimd.collective_compute(kind, ALU.bypass, replica_groups=[list(range(NCORES))], ins=[src.opt()], outs=[dst.opt()])
        s[1] += 1
        ins.then_inc(s[0], 1)
        tok = ("dma", s[0], s[1], "d_" + semname)
        self._record(tok, reads, writes)
        return ins

    def wait_all(self, e):
        for k in self.eng:
            if self.cnt[k] > 0:
                self._wait(e, ("eng", self.sem[k], self.cnt[k], k))
        for name, s in self.dsems.items():
            if s[1] > 0:
                self._wait(e, ("dma", s[0], s[1], "d_" + name))

    def barrier(self, engines=("pe", "act", "dve", "pool", "sp")):
        for e in engines:
            self.wait_all(e)


class Env:
    def __init__(self, nc, mk, es, consts_dram):
        self.nc, self.mk, self.es = nc, mk, es
        self.pb = [es.enter_context(nc.psum_tensor(f"pb{i}", [128, 512], F32)) for i in range(8)]
        self.bpb = [mk.buf(f"pb{i}") for i in range(8)]
        self.cf = es.enter_context(nc.sbuf_tensor("constf", [128, NCONST * 128], F32))
        self.cb = es.enter_context(nc.sbuf_tensor("constb", [128, NCONST * 128], BF16))
        self.onesf = es.enter_context(nc.sbuf_tensor("onesf", [128, 128], F32))
        self.onesb = es.enter_context(nc.sbuf_tensor("onesb", [128, 128], BF16))
        self.bconst = mk.buf("const")
        mk.dma("sp", self.cf[:], consts_dram, "const", writes=[self.bconst])
        mk.op("dve", lambda e: e.tensor_copy(out=self.cb[:], in_=self.cf[:]), reads=[self.bconst], writes=[self.bconst])
        mk.op("dve", lambda e: e.memset(self.onesf[:], 1.0), writes=[self.bconst])
        mk.op("dve", lambda e: e.memset(self.onesb[:], 1.0), writes=[self.bconst])
        self.epsc = es.enter_context(nc.sbuf_tensor("epsc", [128, 1], F32))
        mk.op("dve", lambda e: e.memset(self.epsc[:], EPS), writes=[self.bconst])
        self.identf = self.cf[:, 0:128]
        self.identb = self.cb[:, 0:128]

    def cF(self, i):
        return self.cf[:, i * 128:(i + 1) * 128]

    def cB(self, i):
        return self.cb[:, i * 128:(i + 1) * 128]

    def sb(self, es, name, shape, dt):
        self.uid = getattr(self, "uid", 0) + 1
        return es.enter_context(self.nc.sbuf_tensor(f"{name}_u{self.uid}", shape, dt))


NCONST = 12
C_ID, C_U, C_UT, C_L, C_LT, C_U64, C_UT64, C_L64, C_LT64, C_BD, C_H0, C_H1 = range(12)


def make_consts():
    i = np.arange(128)
    m, l = i[:, None], i[None, :]
    same = (m // 64) == (l // 64)
    mats = [m == l, m <= l, m >= l, m > l, m < l,
            (m <= l) & same, (m >= l) & same, (m > l) & same, (m < l) & same, same,
            (m < 64) & (l >= 0), (m >= 64) & (l >= 0)]
    return np.concatenate(mats, axis=1).astype(np.float32)


def rstd_from_ss(env, es, ss_banks, ntok, name, n=D, pre=None):
    mk = env.mk
    if pre is None:
        rs = env.sb(es, name + "_rs", [128, ntok], F32)
        rstd = env.sb(es, name + "_rstd", [128, ntok], F32)
        brs, brstd = mk.buf(), mk.buf()
    else:
        rs, rstd, brs, brstd = pre
    for h, bi in enumerate(ss_banks):
        sl = slice(h * 512, (h + 1) * 512)
        mk.op("act", lambda e: e.activation(out=rs[:, sl], in_=env.pb[bi][:], func=AF.Sqrt, bias=env.epsc[:, 0:1], scale=1.0 / n),
              reads=[env.bconst], xreads=[env.bpb[bi]], writes=[brs])
    mk.op("dve", lambda e: e.reciprocal(out=rstd[:], in_=rs[:]), reads=[brs], writes=[brstd])
    return rstd, brstd


def phase_prenorm(env, es_out, x_src, gcol, bg, hT, hB):
    mk, nc = env.mk, env.nc
    with ExitStack() as es:
        NR = 4
        xr = [env.sb(es, f"xr{i}", [128, NT], F32) for i in range(NR)]
        bx = [mk.buf() for _ in range(NR)]
        sq = [env.sb(es, f"sq{i}", [128, NT], F32) for i in range(2)]
        bsq = [mk.buf() for _ in range(2)]

        def load(i):
            c = i % KC
            mk.dma("sp", xr[i % NR][:], x_src[:, c, :], f"xr{i % NR}", writes=[bx[i % NR]])

        for i in range(NR - 1):
            load(i)
        rstd = brstd = None
        for i in range(2 * KC):
            if i + NR - 1 < 2 * KC:
                load(i + NR - 1)
            c, r = i % KC, i % NR
            if i < KC:
                s = c % 2
                mk.op("act", lambda e: e.activation(out=sq[s][:], in_=xr[r][:], func=AF.Square), reads=[bx[r]], writes=[bsq[s]])
                for h in range(NT // 512):
                    mk.op("pe", lambda e: e.matmul(env.pb[6 + h][:], lhsT=env.onesf[:], rhs=sq[s][:, h * 512:(h + 1) * 512],
                                                   start=(c == 0), stop=(c == KC - 1)),
                          reads=[bsq[s], env.bconst], writes=[env.bpb[6 + h]])
            else:
                if rstd is None:
                    rstd, brstd = rstd_from_ss(env, es, [6, 7], NT, "pre")
                mk.op("dve", lambda e: e.scalar_tensor_tensor(out=hT[:, c, :], in0=xr[r][:], scalar=gcol[:, c:c + 1], in1=rstd[:],
                                                              op0=ALU.mult, op1=ALU.mult),
                      reads=[bx[r], brstd, bg], writes=[hB[c]])
        mk.barrier()


def phase_postnorm_update(env, es, zT, zB, ss_banks, x_src, x_dst, bxdst, name):
    mk = env.mk
    rstd, brstd = rstd_from_ss(env, es, ss_banks, NT, name)
    xin = [env.sb(es, f"{name}_xin{i}", [128, NT], F32) for i in range(3)]
    tmp = [env.sb(es, f"{name}_tmp{i}", [128, NT], F32) for i in range(2)]
    bxin = [mk.buf() for _ in range(3)]
    btmp = [mk.buf() for _ in range(2)]
    for c in range(KC):
        s3, s2 = c % 3, c % 2
        mk.dma("sp", xin[s3][:], x_src[:, c, :], f"{name}_xin{s3}", writes=[bxin[s3]])
        mk.op("dve", lambda e: e.tensor_tensor(out=tmp[s2][:], in0=zT[:, c, :], in1=rstd[:], op=ALU.mult),
              reads=[zB[c], brstd], writes=[btmp[s2]])
        mk.op("dve", lambda e: e.tensor_tensor(out=xin[s3][:], in0=xin[s3][:], in1=tmp[s2][:], op=ALU.add),
              reads=[btmp[s2], bxin[s3]], writes=[bxin[s3]])
        mk.dma("sp", x_dst[:, c, :], xin[s3][:], f"{name}_xout", reads=[bxin[s3]], writes=[bxdst])


def proj_accum_post(env, es, nk, wtile_src, rhsT, rhsB, gcol, bg, x_src, x_dst, bxdst, name, wbufs=2):
    mk = env.mk
    zT = env.sb(es, name + "_zT", [128, KC, NT], BF16)
    zB = [mk.buf() for _ in range(KC)]
    wt = [env.sb(es, f"{name}_w{i}", [128, nk, 128], BF16) for i in range(wbufs)]
    bw = [mk.buf() for _ in range(wbufs)]
    fsq = [env.sb(es, f"{name}_fsq{i}", [128, 512], F32) for i in range(2)]
    bfsq = [mk.buf() for _ in range(2)]
    nh = NT // 512

    def load(c):
        s = c % wbufs
        mk.dma("pool", wt[s][:], wtile_src(c), f"{name}_w{s}", writes=[bw[s]])

    for c in range(min(wbufs - 1, KC)):
        load(c)
    it = 0
    for c in range(KC):
        if c + wbufs - 1 < KC:
            load(c + wbufs - 1)
        s = c % wbufs
        for h in range(nh):
            bi = it % 4
            sl = slice(h * 512, (h + 1) * 512)
            for k in range(nk):
                mk.op("pe", lambda e: e.matmul(env.pb[bi][:], lhsT=wt[s][:, k, :], rhs=rhsT[:, k, sl], start=(k == 0), stop=(k == nk - 1)),
                      reads=[bw[s], rhsB[k]], writes=[env.bpb[bi]])
            q = it % 2
            mk.op("act", lambda e: e.activation(out=fsq[q][:], in_=env.pb[bi][:], func=AF.Square), xreads=[env.bpb[bi]], writes=[bfsq[q]])
            mk.op("dve", lambda e: e.tensor_scalar(out=zT[:, c, sl], in0=env.pb[bi][:], scalar1=gcol[:, c:c + 1], scalar2=None, op0=ALU.mult),
                  reads=[bg], xreads=[env.bpb[bi]], writes=[zB[c]])
            if os.environ.get("K1") is None or c == 0:
                mk.op("pe", lambda e: e.matmul(env.pb[4 + h][:], lhsT=env.onesf[:], rhs=fsq[q][:], start=(c == 0), stop=(c == KC - 1)),
                      reads=[bfsq[q], env.bconst], writes=[env.bpb[4 + h]])
            it += 1
    if os.environ.get("K2") is None:
        phase_postnorm_update(env, es, zT, zB, [4 + h for h in range(nh)], x_src, x_dst, bxdst, name)


def phase_ffn(env, x_src, x_dst, bxdst, w_gu, w_down, gpre, gpost_half, bg):
    mk, nc = env.mk, env.nc
    with ExitStack() as es:
        actT = env.sb(es, "actT", [128, FC, NT], BF16)
        aB = [mk.buf() for _ in range(FC)]
        with ExitStack() as es2:
            hT = env.sb(es2, "hT", [128, KC, NT], BF16)
            hB = [mk.buf() for _ in range(KC)]
            phase_prenorm(env, es2, x_src, gpre, bg, hT, hB)
            if os.environ.get("STAGE") == "1":
                return
            NB = 3
            wg = [env.sb(es2, f"wg{i}", [128, KC, 128], BF16) for i in range(NB)]
            wu = [env.sb(es2, f"wu{i}", [128, KC, 128], BF16) for i in range(NB)]
            bwt = [mk.buf() for _ in range(NB)]
            sg = [env.sb(es2, f"sg{i}", [128, 512], F32) for i in range(2)]
            bsg = [mk.buf() for _ in range(2)]

            def load(j):
                s = j % NB
                mk.dma("pool", wg[s][:], w_gu[:, j * 128:(j + 1) * 128].rearrange("(k p) c -> p k c", p=128), f"wgu{s}", writes=[bwt[s]])
                mk.dma("pool", wu[s][:], w_gu[:, DFF + j * 128:DFF + (j + 1) * 128].rearrange("(k p) c -> p k c", p=128), f"wgu{s}", writes=[bwt[s]])

            for j in range(NB - 1):
                load(j)
            it = 0
            for j in range(FC):
                if j + NB - 1 < FC:
                    load(j + NB - 1)
                s = j % NB
                for h in range(NT // 512):
                    sl = slice(h * 512, (h + 1) * 512)
                    bgi, bui = 2 * (it % 2), 2 * (it % 2) + 1
                    for k in range(KC):
                        mk.op("pe", lambda e: e.matmul(env.pb[bgi][:], lhsT=wg[s][:, k, :], rhs=hT[:, k, sl], start=(k == 0), stop=(k == KC - 1)),
                              reads=[bwt[s], hB[k]], writes=[env.bpb[bgi]])
                    for k in range(KC):
                        mk.op("pe", lambda e: e.matmul(env.pb[bui][:], lhsT=wu[s][:, k, :], rhs=hT[:, k, sl], start=(k == 0), stop=(k == KC - 1)),
                              reads=[bwt[s], hB[k]], writes=[env.bpb[bui]])
                    q = it % 2
                    mk.op("act", lambda e: e.activation(out=sg[q][:], in_=env.pb[bgi][:], func=AF.Silu), xreads=[env.bpb[bgi]], writes=[bsg[q]])
                    mk.op("dve", lambda e: e.tensor_tensor(out=actT[:, j, sl], in0=sg[q][:], in1=env.pb[bui][:], op=ALU.mult),
                          reads=[bsg[q]], xreads=[env.bpb[bui]], writes=[aB[j]])
                    it += 1
            mk.barrier()
        if os.environ.get("STAGE") == "2":
            return
        with ExitStack() as es3:
            proj_accum_post(env, es3, FC, lambda c: w_down[:, c * 128:(c + 1) * 128].rearrange("(j p) c -> p j c", p=128),
                            actT, aB, gpost_half, bg, x_src, x_dst, bxdst, "dn")
            mk.barrier()


def phase_ma(env, x_src, gcol, bg, hT_dst, bhdst):
    mk = env.mk
    with ExitStack() as es:
        hT = env.sb(es, "hTma", [128, KC, NT], BF16)
        hB = [mk.buf() for _ in range(KC)]
        phase_prenorm(env, es, x_src, gcol, bg, hT, hB)
        for c in range(KC):
            mk.dma("sp", hT_dst[:, c, :], hT[:, c, :], "hTout", reads=[hB[c]], writes=[bhdst])
        mk.barrier()


def phase_mc(env, x_src, x_dst, bxdst, hT_src, yT_src, w_in, w_out, gs, bg, zcols=None, y_reads=()):
    mk = env.mk
    nh = NT // 512
    with ExitStack() as es:
        yT = env.sb(es, "yT", [128, 32, NT], BF16)
        yB = [mk.buf() for _ in range(32)]
        for c in range(32):
            mk.dma("sp", yT[:, c, :], yT_src(c) if callable(yT_src) else yT_src[:, c, :], "yTin", reads=y_reads, writes=[yB[c]])
        with ExitStack() as es2:
            hT = env.sb(es2, "hTmc", [128, KC, NT], BF16)
            hB = [mk.buf() for _ in range(KC)]
            for c in range(KC):
                mk.dma("sp", hT[:, c, :], hT_src[:, c, :], "hTin", writes=[hB[c]])
            wz = [env.sb(es2, f"wz{i}", [128, KC, 128], BF16) for i in range(3)]
            bwz = [mk.buf() for _ in range(3)]
            uu = [env.sb(es2, f"uu{i}", [128, NT], F32) for i in range(4)]
            buu = [mk.buf() for _ in range(4)]
            sz = [env.sb(es2, f"sz{i}", [128, NT], F32) for i in range(2)]
            bsz = [mk.buf() for _ in range(2)]
            sq = [env.sb(es2, f"sqm{i}", [128, NT], F32) for i in range(2)]
            bsq = [mk.buf() for _ in range(2)]
            if zcols is None:
                zcols = [i * 128 for i in range(16)] + [11328 + i * 128 for i in range(16)]
            pre = (env.sb(es2, "mc_rs", [128, NT], F32), env.sb(es2, "mc_rstd", [128, NT], F32), mk.buf(), mk.buf())

            def loadz(i):
                mk.dma("pool", wz[i % 3][:], w_in[:, zcols[i]:zcols[i] + 128].rearrange("(k p) c -> p k c", p=128), f"wz{i % 3}", writes=[bwz[i % 3]])

            def zproj(i):
                s = i % 3
                q = i % 2
                for h in range(nh):
                    bi = (2 * i + h) % 4
                    sl = slice(h * 512, (h + 1) * 512)
                    for k in range(KC):
                        mk.op("pe", lambda e: e.matmul(env.pb[bi][:], lhsT=wz[s][:, k, :], rhs=hT[:, k, sl], start=(k == 0), stop=(k == KC - 1)),
                              reads=[bwz[s], hB[k]], writes=[env.bpb[bi]])
                    mk.op("act", lambda e: e.activation(out=sz[q][:, sl], in_=env.pb[bi][:], func=AF.Silu), xreads=[env.bpb[bi]], writes=[bsz[q]])
                return sz[q], bsz[q]

            loadz(0)
            loadz(1)
            for gi in range(4):
                for j in range(4):
                    i = gi * 4 + j
                    if i + 2 < 32:
                        loadz(i + 2)
                    szt, bszt = zproj(i)
                    mk.op("dve", lambda e: e.tensor_tensor(out=uu[j][:], in0=yT[:, i, :], in1=szt[:], op=ALU.mult), reads=[yB[i], bszt], writes=[buu[j]])
                    q = i % 2
                    mk.op("act", lambda e: e.activation(out=sq[q][:], in_=uu[j][:], func=AF.Square), reads=[buu[j]], writes=[bsq[q]])
                    for h in range(nh):
                        mk.op("pe", lambda e: e.matmul(env.pb[6 + h][:], lhsT=env.onesf[:], rhs=sq[q][:, h * 512:(h + 1) * 512], start=(j == 0), stop=(j == 3)),
                              reads=[bsq[q], env.bconst], writes=[env.bpb[6 + h]])
                rstd, brstd = rstd_from_ss(env, es2, [6 + h for h in range(nh)], NT, "sg", n=512, pre=pre)
                for j in range(4):
                    i = gi * 4 + j
                    mk.op("dve", lambda e: e.scalar_tensor_tensor(out=yT[:, i, :], in0=uu[j][:], scalar=gs[:, i:i + 1], in1=rstd[:], op0=ALU.mult, op1=ALU.mult),
                          reads=[buu[j], brstd, bg], writes=[yB[i]])
            for hd in range(16):
                i = 16 + hd
                if i + 2 < 32:
                    loadz(i + 2)
                q = i % 2
                mk.op("act", lambda e: e.activation(out=sq[q][:], in_=yT[:, i, :], func=AF.Square), reads=[yB[i]], writes=[bsq[q]])
                for h in range(nh):
                    mk.op("pe", lambda e: e.matmul(env.pb[6 + h][:], lhsT=env.onesf[:], rhs=sq[q][:, h * 512:(h + 1) * 512], start=True, stop=True),
                          reads=[bsq[q], env.bconst], writes=[env.bpb[6 + h]])
                szt, bszt = zproj(i)
                rstd, brstd = rstd_from_ss(env, es2, [6 + h for h in range(nh)], NT, "gg", n=128, pre=pre)
                j = hd % 4
                mk.op("dve", lambda e: e.scalar_tensor_tensor(out=uu[j][:], in0=yT[:, i, :], scalar=gs[:, 16:17], in1=rstd[:], op0=ALU.mult, op1=ALU.mult),
                      reads=[yB[i], brstd, bg], writes=[buu[j]])
                mk.op("dve", lambda e: e.tensor_tensor(out=yT[:, i, :], in0=uu[j][:], in1=szt[:], op=ALU.mult), reads=[buu[j], bszt], writes=[yB[i]])
            mk.barrier()
        with ExitStack() as es4:
            proj_accum_post(env, es4, 32, lambda c: w_out[:, c * 128:(c + 1) * 128].rearrange("(j p) c -> p j c", p=128),
                            yT, yB, gs[:, 17:33], bg, x_src, x_dst, bxdst, "op")
            mk.barrier()


NBLK = SEQ // 128
NTILE = SEQ // 512
QSCALE = 128 ** -0.5


def conv_silu(env, win_ap, W, cw, ci, acc, bacc, reads, dve="dve"):
    mk = env.mk
    mk.op(dve, lambda e: e.tensor_scalar(out=acc[:, 0:W], in0=win_ap[:, 0:W], scalar1=cw[:, ci, 0:1], scalar2=None, op0=ALU.mult),
          reads=reads, writes=[bacc])
    for j in range(1, 5):
        mk.op("dve", lambda e: e.scalar_tensor_tensor(out=acc[:, 0:W], in0=win_ap[:, j:j + W], scalar=cw[:, ci, j:j + 1], in1=acc[:, 0:W],
                                                      op0=ALU.mult, op1=ALU.add),
              reads=reads, writes=[bacc])


def inproj_conv_pass(env, hTall, wmb, col0, nch, cw, bcw, cw_idx, pc, pcB, l2, small=None, hT_reads=()):
    mk, nc = env.mk, env.nc
    with ExitStack() as es:
        W = env.sb(es, "W", [128, KC, nch * 128], BF16)
        bW = mk.buf()
        for ci in range(nch):
            mk.dma("pool", W[:, :, ci * 128:(ci + 1) * 128],
                   wmb[:, col0 + ci * 128:col0 + (ci + 1) * 128].rearrange("(k p) c -> p k c", p=128), "W", writes=[bW])
        if small is not None:
            col0s, nsm, small_raw, bsmall = small
            Wsm = env.sb(es, "Wsm", [128, KC, nsm], BF16)
            mk.dma("pool", Wsm[:], wmb[:, col0s:col0s + nsm].rearrange("(k p) c -> p k c", p=128), "W", writes=[bW])
        ht = [env.sb(es, f"ht{i}", [128, KC, 512], BF16) for i in range(2)]
        bht = [mk.buf() for _ in range(2)]
        win = [[env.sb(es, f"win{p}_{ci}", [128, 516], F32) for ci in range(nch)] for p in range(2)]
        bwin = [[mk.buf() for ci in range(nch)] for p in range(2)]
        acc = [env.sb(es, f"acc{i}", [128, 512], F32) for i in range(2)]
        bacc = [mk.buf() for _ in range(2)]
        sl = [env.sb(es, f"sl{i}", [128, 512], F32) for i in range(2)]
        bsl = [mk.buf() for _ in range(2)]
        sq = env.sb(es, "sq", [128, 512], F32)
        rs = env.sb(es, "rs", [128, 512], F32)
        bsq, brs = mk.buf(), mk.buf()
        tail = env.sb(es, "tail", [128, nch, 8], F32)
        btail = mk.buf()
        mk.op("dve", lambda e: e.memset(tail[:], 0.0), writes=[btail])
        for ci in range(nch):
            mk.op("dve", lambda e: e.memset(win[0][ci][:, 0:4], 0.0), writes=[bwin[0][ci]])

        def load(T):
            src = hTall(T) if callable(hTall) else hTall[:, :, T * 512:(T + 1) * 512]
            mk.dma("sp", ht[T % 2][:], src, f"ht{T % 2}", reads=hT_reads, writes=[bht[T % 2]])

        it = [0]

        def epilogue(ci, win_ap, bwin_, W_, lo, dst0):
            q = it[0] % 2
            it[0] += 1
            conv_silu(env, win_ap, W_, cw, cw_idx[ci], acc[q], bacc[q], [bwin_, bcw])
            n = W_ - lo
            if l2[ci] is None:
                mk.op("act", lambda e: e.activation(out=pc[ci][:, dst0:dst0 + n], in_=acc[q][:, lo:W_], func=AF.Silu, bias=cw[:, cw_idx[ci], 5:6]),
                      reads=[bacc[q], bcw], writes=[pcB[ci]])
            else:
                mk.op("act", lambda e: e.activation(out=sl[q][:, 0:n], in_=acc[q][:, lo:W_], func=AF.Silu, bias=cw[:, cw_idx[ci], 5:6]),
                      reads=[bacc[q], bcw], writes=[bsl[q]])
                mk.op("act", lambda e: e.activation(out=sq[:, 0:n], in_=sl[q][:, 0:n], func=AF.Square), reads=[bsl[q]], writes=[bsq])
                mk.op("pe", lambda e: e.matmul(env.pb[5][:, 0:n], lhsT=env.onesf[:], rhs=sq[:, 0:n], start=True, stop=True),
                      reads=[bsq, env.bconst], writes=[env.bpb[5]])
                mk.op("act", lambda e: e.activation(out=rs[:, 0:n], in_=env.pb[5][:, 0:n], func=AF.Sqrt, bias=env.epsc[:, 0:1], scale=1.0),
                      reads=[env.bconst], xreads=[env.bpb[5]], writes=[brs])
                mk.op("dve", lambda e: e.reciprocal(out=rs[:, 0:n], in_=rs[:, 0:n]), reads=[brs], writes=[brs])
                mk.op("dve", lambda e: e.scalar_tensor_tensor(out=pc[ci][:, dst0:dst0 + n], in0=sl[q][:, 0:n], scalar=float(l2[ci]), in1=rs[:, 0:n],
                                                              op0=ALU.mult, op1=ALU.mult),
                      reads=[bsl[q], brs], writes=[pcB[ci]])

        load(0)
        for T in range(NTILE):
            if T + 1 < NTILE:
                load(T + 1)
            p = T % 2
            for ci in range(nch):
                bi = ci % 4
                for k in range(KC):
                    mk.op("pe", lambda e: e.matmul(env.pb[bi][:], lhsT=W[:, k, ci * 128:(ci + 1) * 128], rhs=ht[p][:, k, :],
                                                   start=(k == 0), stop=(k == KC - 1)),
                          reads=[bW, bht[p]], writes=[env.bpb[bi]])
                mk.op("act", lambda e: e.copy(out=win[p][ci][:, 4:516], in_=env.pb[bi][:]), xreads=[env.bpb[bi]], writes=[bwin[p][ci]])
                if T > 0:
                    mk.op("act", lambda e: e.copy(out=win[p][ci][:, 0:4], in_=win[1 - p][ci][:, 512:516]), reads=[bwin[1 - p][ci]], writes=[bwin[p][ci]])
                if T == 0:
                    epilogue(ci, win[p][ci], bwin[p][ci], 512, 2, 0)
                else:
                    epilogue(ci, win[p][ci], bwin[p][ci], 512, 0, T * 512 - 2)
            if small is not None:
                for blk in range(4):
                    for k in range(KC):
                        mk.op("pe", lambda e: e.matmul(env.pb[4][:, blk * nsm:(blk + 1) * nsm], lhsT=ht[p][:, k, blk * 128:(blk + 1) * 128], rhs=Wsm[:, k, :],
                                                       start=(k == 0), stop=(k == KC - 1)),
                              reads=[bW, bht[p]], writes=[env.bpb[4]])
                mk.op("act", lambda e: e.copy(out=small_raw[:, T * 4:(T + 1) * 4, :], in_=env.pb[4][:, 0:4 * nsm].rearrange("p (b n) -> p b n", n=nsm)),
                      xreads=[env.bpb[4]], writes=[bsmall])
        p = (NTILE - 1) % 2
        for ci in range(nch):
            mk.op("act", lambda e: e.copy(out=tail[:, ci, 0:4], in_=win[p][ci][:, 512:516]), reads=[bwin[p][ci]], writes=[btail])
            epilogue(ci, tail[:, ci, :], btail, 2, 0, SEQ - 2)
        mk.barrier()


def softplus(env, x_ap, bx, n_shape_tmp=None):
    mk = env.mk
    mk.op("act", lambda e: e.activation(out=x_ap, in_=x_ap, func=AF.Exp), reads=[], writes=[bx])
    mk.op("act", lambda e: e.activation(out=x_ap, in_=x_ap, func=AF.Ln, bias=env.onesf[:, 0:1], scale=1.0), reads=[env.bconst], writes=[bx])


def bc(ap, shape, axis):
    return ap.unsqueeze(axis).to_broadcast(shape)


def ssd_scan(env, pc, pcB, small_raw, bsmall, sp, bsp, ymb, bymb):
    mk, nc = env.mk, env.nc
    with ExitStack() as es:
        f32t = lambda n, sh: env.sb(es, n, sh, F32)
        dt = f32t("dt", [128, NBLK, 8]); A = f32t("A", [128, NBLK, 8]); cs = f32t("cs", [128, NBLK, 8]); tot = f32t("tot", [128, NBLK, 8])
        ecs = f32t("ecs", [128, NBLK, 8]); cdec = f32t("cdec", [128, NBLK, 8]); dte = f32t("dte", [128, NBLK, 8])
        aneg = f32t("aneg", [128, 8])
        bdt, bA, bcs, btot, becs, bcdec, bdte, baneg = [mk.buf() for _ in range(8)]
        mk.op("dve", lambda e: e.tensor_tensor(out=dt[:], in0=small_raw[:, :, 0:8], in1=bc(sp[:, 0:8], [128, NBLK, 8], 1), op=ALU.add),
              reads=[bsmall, bsp], writes=[bdt])
        softplus(env, dt[:], bdt)
        mk.op("act", lambda e: e.activation(out=aneg[:], in_=sp[:, 8:16], func=AF.Exp), reads=[bsp], writes=[baneg])
        mk.op("dve", lambda e: e.scalar_tensor_tensor(out=A[:], in0=dt[:], scalar=-1.0, in1=bc(aneg[:], [128, NBLK, 8], 1), op0=ALU.mult, op1=ALU.mult),
              reads=[bdt, baneg], writes=[bA])
        for c in range(NBLK):
            mk.op("pe", lambda e: e.matmul(env.pb[7][:, c * 8:c * 8 + 4], lhsT=env.cF(C_U), rhs=A[:, c, 0:4], start=True, stop=True),
                  reads=[bA, env.bconst], writes=[env.bpb[7]])
            mk.op("pe", lambda e: e.matmul(env.pb[7][:, c * 8 + 4:c * 8 + 8], lhsT=env.cF(C_UT), rhs=A[:, c, 4:8], start=True, stop=True),
                  reads=[bA, env.bconst], writes=[env.bpb[7]])
            mk.op("pe", lambda e: e.matmul(env.pb[6][:, c * 8:c * 8 + 8], lhsT=env.onesf[:], rhs=A[:, c, :], start=True, stop=True),
                  reads=[bA, env.bconst], writes=[env.bpb[6]])
        mk.op("act", lambda e: e.copy(out=cs[:].rearrange("p b n -> p (b n)"), in_=env.pb[7][:]), xreads=[env.bpb[7]], writes=[bcs])
        mk.op("act", lambda e: e.copy(out=tot[:].rearrange("p b n -> p (b n)"), in_=env.pb[6][:]), xreads=[env.bpb[6]], writes=[btot])
        mk.op("act", lambda e: e.activation(out=ecs[:], in_=cs[:], func=AF.Exp), reads=[bcs], writes=[becs])
        mk.op("act", lambda e: e.activation(out=cdec[:], in_=tot[:], func=AF.Exp), reads=[btot], writes=[bcdec])
        mk.op("dve", lambda e: e.tensor_tensor(out=dte[:], in0=tot[:], in1=cs[:], op=ALU.subtract), reads=[btot, bcs], writes=[bdte])
        mk.op("act", lambda e: e.activation(out=dte[:], in_=dte[:], func=AF.Exp), reads=[], writes=[bdte])
        dI = env.sb(es, "dI", [128, 4, 128], BF16)
        bdI = mk.buf()
        for h in range(4):
            mk.op("dve", lambda e: e.tensor_scalar(out=dI[:, h, :], in0=env.cF(C_ID), scalar1=sp[:, 16 + h:17 + h], scalar2=None, op0=ALU.mult),
                  reads=[bsp, env.bconst], writes=[bdI])
        yacc = f32t("yacc", [128, NBLK, 256])
        byacc = [mk.buf() for _ in range(NBLK)]
        hstF = [f32t(f"hstF{d}", [128, 256]) for d in range(2)]
        hstB = [env.sb(es, f"hstB{d}", [128, 256], BF16) for d in range(2)]
        bhF = [mk.buf() for _ in range(2)]
        bhB = [mk.buf() for _ in range(2)]
        for d in range(2):
            mk.op("dve", lambda e: e.memset(hstF[d][:], 0.0), writes=[bhF[d]])
            mk.op("dve", lambda e: e.memset(hstB[d][:], 0.0), writes=[bhB[d]])
        NB2 = 2
        tm = [env.sb(es, f"tm{i}", [128, 384], BF16) for i in range(NB2)]
        Gm = [f32t(f"Gm{i}", [128, 128]) for i in range(NB2)]
        aL = [f32t(f"aL{i}", [128, 4, 128]) for i in range(NB2)]
        E = [f32t(f"E{i}", [128, 4, 128]) for i in range(NB2)]
        WT = [env.sb(es, f"WT{i}", [128, 4, 128], BF16) for i in range(NB2)]
        xdt = [env.sb(es, f"xdt{i}", [128, 4, 64], BF16) for i in range(NB2)]
        xdte = [env.sb(es, f"xdte{i}", [128, 4, 64], BF16) for i in range(NB2)]
        yo = [f32t(f"yo{i}", [128, 4, 64]) for i in range(NB2)]
        t2 = [f32t(f"t2{i}", [128, 256]) for i in range(NB2)]
        btm, bGm, baL, bE, bWT, bxdt, bxdte, byo, bt2 = [[mk.buf() for _ in range(NB2)] for _ in range(9)]
        touched = [False] * NBLK
        def pipeline(d):
            BA_, BB_, BC_, BD_ = 4 * d, 4 * d + 1, 4 * d + 2, 4 * d + 3
            for step in range(NBLK):
                c = step if d == 0 else NBLK - 1 - step
                cols = slice(c * 128, (c + 1) * 128)
                q = d
                tps = env.pb[BA_][:].bitcast(BF16)
                mk.op("pe", lambda e: e.transpose(out=tps[:, 0:128], in_=pc[2][:, cols], identity=env.cB(C_ID)), reads=[pcB[2], env.bconst], writes=[env.bpb[BA_]])
                mk.op("pe", lambda e: e.transpose(out=tps[:, 128:256], in_=pc[0][:, cols], identity=env.cB(C_ID)), reads=[pcB[0], env.bconst], writes=[env.bpb[BA_]])
                mk.op("pe", lambda e: e.transpose(out=tps[:, 256:384], in_=pc[1][:, cols], identity=env.cB(C_ID)), reads=[pcB[1], env.bconst], writes=[env.bpb[BA_]])
                mk.op("act", lambda e: e.copy(out=tm[q][:], in_=tps[:, 0:384]), xreads=[env.bpb[BA_]], writes=[btm[q]])
                yield
                mk.op("pe", lambda e: e.matmul(env.pb[BB_][:, 0:128], lhsT=pc[2][:, cols], rhs=pc[3][:, cols], start=True, stop=True),
                      reads=[pcB[2], pcB[3]], writes=[env.bpb[BB_]])
                mk.op("dve", lambda e: e.tensor_tensor(out=Gm[q][:], in0=env.pb[BB_][:, 0:128], in1=env.cF(C_U if d == 0 else C_UT), op=ALU.mult),
                      reads=[env.bconst], xreads=[env.bpb[BB_]], writes=[bGm[q]])
                yield
                mk.op("dve", lambda e: e.tensor_tensor(out=aL[q][:], in0=bc(env.cF(C_L if d == 0 else C_LT), [128, 4, 128], 1),
                                                       in1=bc(A[:, c, d * 4:d * 4 + 4], [128, 4, 128], 2), op=ALU.mult),
                      reads=[bA, env.bconst], writes=[baL[q]])
                for h in range(4):
                    mk.op("pe", lambda e: e.matmul(env.pb[BC_][:, h * 128:(h + 1) * 128], lhsT=aL[q][:, h, :], rhs=env.cF(C_U if d == 0 else C_UT), start=True, stop=True),
                          reads=[baL[q], env.bconst], writes=[env.bpb[BC_]])
                yield
                mk.op("act", lambda e: e.activation(out=E[q][:].rearrange("p h l -> p (h l)"), in_=env.pb[BC_][:], func=AF.Exp), xreads=[env.bpb[BC_]], writes=[bE[q]])
                mk.op("dve", lambda e: e.tensor_tensor(out=WT[q][:], in0=E[q][:], in1=bc(Gm[q][:], [128, 4, 128], 1), op=ALU.mult),
                      reads=[bE[q], bGm[q]], writes=[bWT[q]])
                yield
                xtm = tm[q][:, 128:384].rearrange("p (h e) -> p h e", e=64)
                mk.op("dve", lambda e: e.tensor_tensor(out=xdt[q][:], in0=xtm, in1=bc(dt[:, c, d * 4:d * 4 + 4], [128, 4, 64], 2), op=ALU.mult),
                      reads=[btm[q], bdt], writes=[bxdt[q]])
                mk.op("dve", lambda e: e.tensor_tensor(out=xdte[q][:], in0=xdt[q][:], in1=bc(dte[:, c, d * 4:d * 4 + 4], [128, 4, 64], 2), op=ALU.mult),
                      reads=[bxdt[q], bdte], writes=[bxdte[q]])
                yield
                for h in range(4):
                    mk.op("pe", lambda e: e.matmul(env.pb[BD_][:, h * 64:(h + 1) * 64], lhsT=WT[q][:, h, :], rhs=xdt[q][:, h, :], start=True, stop=(d == 1)),
                          reads=[bWT[q], bxdt[q]], writes=[env.bpb[BD_]])
                    if d == 0:
                        mk.op("pe", lambda e: e.matmul(env.pb[BD_][:, h * 64:(h + 1) * 64], lhsT=dI[:, h, :], rhs=tm[q][:, 128 + h * 64:128 + (h + 1) * 64],
                                                       start=False, stop=True),
                              reads=[bdI, btm[q]], writes=[env.bpb[BD_]])
                yield
                mk.op("pe", lambda e: e.matmul(env.pb[BA_][:, 0:256], lhsT=pc[3][:, cols], rhs=hstB[d][:], start=True, stop=True),
                      reads=[pcB[3], bhB[d]], writes=[env.bpb[BA_]])
                for h in range(4):
                    mk.op("act", lambda e: e.activation(out=yo[q][:, h, :], in_=env.pb[BA_][:, h * 64:(h + 1) * 64], func=AF.Copy, scale=ecs[:, c, d * 4 + h:d * 4 + h + 1]),
                          reads=[becs], xreads=[env.bpb[BA_]], writes=[byo[q]])
                if not touched[c]:
                    touched[c] = True
                    mk.op("dve", lambda e: e.tensor_tensor(out=yacc[:, c, :], in0=env.pb[BD_][:, 0:256], in1=yo[q][:].rearrange("p h e -> p (h e)"), op=ALU.add),
                          reads=[byo[q]], xreads=[env.bpb[BD_]], writes=[byacc[c]])
                else:
                    mk.op("dve", lambda e: e.tensor_tensor(out=t2[q][:], in0=env.pb[BD_][:, 0:256], in1=yo[q][:].rearrange("p h e -> p (h e)"), op=ALU.add),
                          reads=[byo[q]], xreads=[env.bpb[BD_]], writes=[bt2[q]])
                    mk.op("dve", lambda e: e.tensor_tensor(out=yacc[:, c, :], in0=yacc[:, c, :], in1=t2[q][:], op=ALU.add),
                          reads=[bt2[q]], writes=[byacc[c]])
                yield
                mk.op("pe", lambda e: e.matmul(env.pb[BB_][:, 0:256], lhsT=tm[q][:, 0:128], rhs=xdte[q][:].rearrange("p h e -> p (h e)"), start=True, stop=True),
                      reads=[btm[q], bxdte[q]], writes=[env.bpb[BB_]])
                mk.op("dve", lambda e: e.tensor_tensor(out=hstF[d][:].rearrange("p (h e) -> p h e", e=64), in0=hstF[d][:].rearrange("p (h e) -> p h e", e=64),
                                                       in1=bc(cdec[:, c, d * 4:d * 4 + 4], [128, 4, 64], 2), op=ALU.mult),
                      reads=[bcdec], writes=[bhF[d]])
                mk.op("dve", lambda e: e.tensor_tensor(out=hstF[d][:], in0=hstF[d][:], in1=env.pb[BB_][:, 0:256], op=ALU.add),
                      xreads=[env.bpb[BB_]], writes=[bhF[d]])
                mk.op("act", lambda e: e.copy(out=hstB[d][:], in_=hstF[d][:]), reads=[bhF[d]], writes=[bhB[d]])
                yield
        alive = [pipeline(0), pipeline(1)]
        while alive:
            for gnr in list(alive):
                try:
                    next(gnr)
                except StopIteration:
                    alive.remove(gnr)
        stg = [env.sb(es, f"stg{i}", [128, 512], BF16) for i in range(2)]
        bstg = [mk.buf() for _ in range(2)]
        it = 0
        for g4 in range(NBLK // 4):
            for half in range(2):
                q = it % 2
                bi = 6 + (it % 2)
                it += 1
                for j in range(4):
                    c = g4 * 4 + j
                    mk.op("pe", lambda e: e.transpose(out=env.pb[bi][:, j * 128:(j + 1) * 128], in_=yacc[:, c, half * 128:(half + 1) * 128], identity=env.cF(C_ID)),
                          reads=[byacc[c], env.bconst], writes=[env.bpb[bi]])
                mk.op("act", lambda e: e.copy(out=stg[q][:], in_=env.pb[bi][:]), xreads=[env.bpb[bi]], writes=[bstg[q]])
                mk.dma("sp", ymb[half, :, g4 * 512:(g4 + 1) * 512], stg[q][:], "ymb", reads=[bstg[q]], writes=[bymb])
        mk.barrier()


def phase_mb(env, es, hTall, wmb, cwd, spd, ymb, do_gdn=True, hT_reads=(), bymb=None):
    mk = env.mk
    cw = env.sb(es, "cw", [128, 10, 6], F32)
    sp = env.sb(es, "sp", [128, 32], F32)
    bcw, bsp = mk.buf(), mk.buf()
    if bymb is None:
        bymb = mk.buf()
    mk.dma("sp", cw[:], cwd, "cw", writes=[bcw])
    mk.dma("sp", sp[:], spd, "sp", writes=[bsp])
    small_raw = env.sb(es, "small_raw", [128, NBLK, 16], F32)
    bsmall = mk.buf()
    with ExitStack() as es2:
        pc = [env.sb(es2, f"pcS{i}", [128, SEQ], BF16) for i in range(4)]
        pcB = [mk.buf() for _ in range(4)]
        inproj_conv_pass(env, hTall, wmb, 0, 4, cw, bcw, [0, 1, 2, 3], pc, pcB, [None] * 4, small=(1280, 16, small_raw, bsmall), hT_reads=hT_reads)
        ssd_scan(env, pc, pcB, small_raw, bsmall, sp, bsp, ymb, bymb)
    if do_gdn:
        with ExitStack() as es3:
            pc = [env.sb(es3, f"pcG{i}", [128, SEQ], BF16) for i in range(6)]
            pcB = [mk.buf() for _ in range(6)]
            inproj_conv_pass(env, hTall, wmb, 512, 6, cw, bcw, [4, 5, 6, 7, 8, 9], pc, pcB, [QSCALE, 1.0, None, QSCALE, 1.0, None], hT_reads=hT_reads)
            gdn_scan(env, pc, pcB, small_raw, bsmall, sp, bsp, ymb, bymb)


def gdn_scan(env, pc, pcB, small_raw, bsmall, sp, bsp, ymb, bymb):
    mk, nc = env.mk, env.nc
    with ExitStack() as es:
        f32t = lambda n, sh: env.sb(es, n, sh, F32)
        bft = lambda n, sh: env.sb(es, n, sh, BF16)
        g = f32t("g", [128, NBLK, 4]); beta = f32t("beta", [128, NBLK, 4]); gcs = f32t("gcs", [128, NBLK, 4]); tot = f32t("tot64", [128, NBLK, 4])
        eg = f32t("eg", [128, NBLK, 4]); nbeta = f32t("nbeta", [128, NBLK, 4]); beg = f32t("beg", [128, NBLK, 4]); kes = f32t("kes", [128, NBLK, 4])
        ld = f32t("ld", [128, NBLK, 2, 4]); an = f32t("an", [128, 4])
        bg_, bbeta, bgcs, btot, beg_, bnbeta, bbeg, bkes, bld, ban = [mk.buf() for _ in range(10)]
        mk.op("dve", lambda e: e.tensor_tensor(out=g[:], in0=small_raw[:, :, 8:12], in1=bc(sp[:, 20:24], [128, NBLK, 4], 1), op=ALU.add),
              reads=[bsmall, bsp], writes=[bg_])
        softplus(env, g[:], bg_)
        mk.op("act", lambda e: e.activation(out=an[:], in_=sp[:, 24:28], func=AF.Exp), reads=[bsp], writes=[ban])
        mk.op("dve", lambda e: e.scalar_tensor_tensor(out=g[:], in0=g[:], scalar=-1.0, in1=bc(an[:], [128, NBLK, 4], 1), op0=ALU.mult, op1=ALU.mult),
              reads=[ban], writes=[bg_])
        mk.op("act", lambda e: e.activation(out=beta[:], in_=small_raw[:, :, 12:16], func=AF.Sigmoid), reads=[bsmall], writes=[bbeta])
        for b in range(NBLK):
            for d in range(2):
                mk.op("pe", lambda e: e.matmul(env.pb[7][:, b * 4 + d * 2:b * 4 + d * 2 + 2], lhsT=env.cF(C_U64 if d == 0 else C_UT64), rhs=g[:, b, d * 2:d * 2 + 2],
                                               start=True, stop=True), reads=[bg_, env.bconst], writes=[env.bpb[7]])
            mk.op("pe", lambda e: e.matmul(env.pb[7][:, 256 + b * 4:256 + b * 4 + 4], lhsT=env.cF(C_BD), rhs=g[:, b, :], start=True, stop=True),
                  reads=[bg_, env.bconst], writes=[env.bpb[7]])
            for hf in range(2):
                mk.op("pe", lambda e: e.matmul(env.pb[6][:, b * 8 + hf * 4:b * 8 + hf * 4 + 4], lhsT=env.cF(C_H0 if hf == 0 else C_H1), rhs=g[:, b, :], start=True, stop=True),
                      reads=[bg_, env.bconst], writes=[env.bpb[6]])
        mk.op("act", lambda e: e.copy(out=gcs[:].rearrange("p b n -> p (b n)"), in_=env.pb[7][:, 0:256]), xreads=[env.bpb[7]], writes=[bgcs])
        mk.op("act", lambda e: e.copy(out=tot[:].rearrange("p b n -> p (b n)"), in_=env.pb[7][:, 256:512]), xreads=[env.bpb[7]], writes=[btot])
        mk.op("act", lambda e: e.activation(out=ld[:].rearrange("p b h n -> p (b h n)"), in_=env.pb[6][:], func=AF.Exp), xreads=[env.bpb[6]], writes=[bld])
        mk.op("act", lambda e: e.activation(out=eg[:], in_=gcs[:], func=AF.Exp), reads=[bgcs], writes=[beg_])
        mk.op("dve", lambda e: e.tensor_scalar(out=nbeta[:], in0=beta[:], scalar1=-1.0, scalar2=None, op0=ALU.mult), reads=[bbeta], writes=[bnbeta])
        mk.op("dve", lambda e: e.tensor_tensor(out=beg[:], in0=beta[:], in1=eg[:], op=ALU.mult), reads=[bbeta, beg_], writes=[bbeg])
        mk.op("dve", lambda e: e.tensor_tensor(out=kes[:], in0=tot[:], in1=gcs[:], op=ALU.subtract), reads=[btot, bgcs], writes=[bkes])
        mk.op("act", lambda e: e.activation(out=kes[:], in_=kes[:], func=AF.Exp), reads=[], writes=[bkes])
        maskp = [f32t(f"maskp{d}", [128, 4, 128]) for d in range(2)]
        bmask = mk.buf()
        for d in range(2):
            for i4, ci in enumerate([C_L64, C_L64, C_U64, C_U64] if d == 0 else [C_LT64, C_LT64, C_UT64, C_UT64]):
                mk.op("dve", lambda e: e.tensor_copy(out=maskp[d][:, i4, :], in_=env.cF(ci)), reads=[env.bconst], writes=[bmask])
        oacc = bft("oacc", [128, NBLK, 256])
        boacc = [[mk.buf() for _ in range(2)] for _ in range(NBLK)]
        Sf = [f32t(f"Sf{d}", [128, 2, 128]) for d in range(2)]
        Sb = [bft(f"Sb{d}", [128, 2, 128]) for d in range(2)]
        bSf = [mk.buf() for _ in range(2)]
        bSb = [mk.buf() for _ in range(2)]
        for d in range(2):
            mk.op("dve", lambda e: e.memset(Sf[d][:], 0.0), writes=[bSf[d]])
            mk.op("dve", lambda e: e.memset(Sb[d][:], 0.0), writes=[bSb[d]])
        NB2 = 2
        mkl = lambda fn: [fn(i) for i in range(NB2)]
        tmk = mkl(lambda i: bft(f"tmk{i}", [128, 4, 128]))
        kbg = mkl(lambda i: f32t(f"kbg{i}", [128, 2, 128])); bv = mkl(lambda i: f32t(f"bv{i}", [128, 2, 128])); ke = [bft(f"ke{i}", [128, 2, 128]) for i in range(4)]
        gUL = mkl(lambda i: f32t(f"gUL{i}", [128, 4, 128])); Eall = mkl(lambda i: f32t(f"Eall{i}", [128, 4, 128]))
        PP = [[f32t(f"PP{i}_{k}", [128, 4, 128]) for k in range(2)] for i in range(NB2)]
        RR = [[f32t(f"RR{i}_{k}", [128, 2, 128]) for k in range(2)] for i in range(NB2)]
        aqT = [bft(f"aqT{i}", [128, 2, 128]) for i in range(4)]; u = [f32t(f"u{i}", [128, 2, 128]) for i in range(4)]; wT = [bft(f"wT{i}", [128, 2, 128]) for i in range(4)]
        vnew = mkl(lambda i: bft(f"vnew{i}", [128, 2, 128])); tt = mkl(lambda i: f32t(f"tt{i}", [128, 2, 128])); t2 = mkl(lambda i: f32t(f"t2_{i}", [128, 2, 128]))
        (btmk, bkbg, bbv, bke, bgUL, bEall, baqT, bu, bwT, bvnew, btt, bt2) = [[mk.buf() for _ in range(NB2)] for _ in range(12)]
        bke, baqT, bu, bwT = [[mk.buf() for _ in range(4)] for _ in range(4)]
        bPP = [[mk.buf() for _ in range(2)] for _ in range(NB2)]
        bRR = [[mk.buf() for _ in range(2)] for _ in range(NB2)]
        touched = [[False, False] for _ in range(NBLK)]
        pre_done = [0, 0]
        chain_done = [0, 0]

        def pre(d):
            PT_, PK_, PD_ = 3 * d, 3 * d + 1, 3 * d + 2
            for step in range(NBLK):
                while step - chain_done[d] >= 1:
                    yield
                par = step % 2
                q2 = 2 * d + par
                b = step if d == 0 else NBLK - 1 - step
                cols = slice(b * 128, (b + 1) * 128)
                q = d
                d2 = slice(d * 2, d * 2 + 2)
                qT = [pc[0][:, cols], pc[3][:, cols]]; kT = [pc[1][:, cols], pc[4][:, cols]]; vT = [pc[2][:, cols], pc[5][:, cols]]
                bq = [pcB[0], pcB[3]]; bk = [pcB[1], pcB[4]]; bvv = [pcB[2], pcB[5]]
                tps = env.pb[PT_][:].bitcast(BF16)
                for j in range(2):
                    mk.op("pe", lambda e: e.transpose(out=tps[:, j * 128:(j + 1) * 128], in_=kT[j], identity=env.cB(C_ID)), reads=[bk[j], env.bconst], writes=[env.bpb[PT_]])
                    mk.op("pe", lambda e: e.transpose(out=tps[:, 256 + j * 128:256 + (j + 1) * 128], in_=vT[j], identity=env.cB(C_ID)), reads=[bvv[j], env.bconst], writes=[env.bpb[PT_]])
                mk.op("act", lambda e: e.copy(out=tmk[q][:].rearrange("p a l -> p (a l)"), in_=tps[:, 0:512]), xreads=[env.bpb[PT_]], writes=[btmk[q]])
                yield
                mk.op("dve", lambda e: e.tensor_tensor(out=kbg[q][:], in0=tmk[q][:, 0:2, :], in1=bc(beg[:, b, d2], [128, 2, 128], 2), op=ALU.mult), reads=[btmk[q], bbeg], writes=[bkbg[q]])
                mk.op("dve", lambda e: e.tensor_tensor(out=bv[q][:], in0=tmk[q][:, 2:4, :], in1=bc(beta[:, b, d2], [128, 2, 128], 2), op=ALU.mult), reads=[btmk[q], bbeta], writes=[bbv[q]])
                mk.op("dve", lambda e: e.tensor_tensor(out=ke[q2][:], in0=tmk[q][:, 0:2, :], in1=bc(kes[:, b, d2], [128, 2, 128], 2), op=ALU.mult), reads=[btmk[q], bkes], writes=[bke[q2]])
                yield
                for j in range(2):
                    mk.op("pe", lambda e: e.matmul(env.pb[PK_][:, j * 128:(j + 1) * 128], lhsT=kT[j], rhs=kT[j], start=True, stop=True), reads=[bk[j]], writes=[env.bpb[PK_]])
                    mk.op("pe", lambda e: e.matmul(env.pb[PK_][:, 256 + j * 128:256 + (j + 1) * 128], lhsT=kT[j], rhs=qT[j], start=True, stop=True), reads=[bk[j], bq[j]], writes=[env.bpb[PK_]])
                yield
                cU, cL = (C_U64, C_L64) if d == 0 else (C_UT64, C_LT64)
                mk.op("dve", lambda e: e.tensor_tensor(out=gUL[q][:, 0:2, :], in0=bc(env.cF(cU), [128, 2, 128], 1), in1=bc(g[:, b, d2], [128, 2, 128], 2), op=ALU.mult),
                      reads=[bg_, env.bconst], writes=[bgUL[q]])
                mk.op("dve", lambda e: e.tensor_tensor(out=gUL[q][:, 2:4, :], in0=bc(env.cF(cL), [128, 2, 128], 1), in1=bc(g[:, b, d2], [128, 2, 128], 2), op=ALU.mult),
                      reads=[bg_, env.bconst], writes=[bgUL[q]])
                for j in range(2):
                    mk.op("pe", lambda e: e.matmul(env.pb[PD_][:, j * 128:(j + 1) * 128], lhsT=gUL[q][:, j, :], rhs=env.cF(cL), start=True, stop=True),
                          reads=[bgUL[q], env.bconst], writes=[env.bpb[PD_]])
                    mk.op("pe", lambda e: e.matmul(env.pb[PD_][:, 256 + j * 128:256 + (j + 1) * 128], lhsT=gUL[q][:, 2 + j, :], rhs=env.cF(cU), start=True, stop=True),
                          reads=[bgUL[q], env.bconst], writes=[env.bpb[PD_]])
                mk.op("act", lambda e: e.activation(out=Eall[q][:].rearrange("p a l -> p (a l)"), in_=env.pb[PD_][:], func=AF.Exp), xreads=[env.bpb[PD_]], writes=[bEall[q]])
                mk.op("dve", lambda e: e.tensor_tensor(out=Eall[q][:], in0=Eall[q][:], in1=maskp[d][:], op=ALU.mult), reads=[bmask], writes=[bEall[q]])
                yield
                for j in range(2):
                    mk.op("dve", lambda e: e.scalar_tensor_tensor(out=PP[q][0][:, 2 + j, :], in0=env.pb[PK_][:, j * 128:(j + 1) * 128], scalar=nbeta[:, b, d * 2 + j:d * 2 + j + 1],
                                                                  in1=Eall[q][:, j, :], op0=ALU.mult, op1=ALU.mult),
                          reads=[bnbeta, bEall[q]], xreads=[env.bpb[PK_]], writes=[bPP[q][0]])
                mk.op("dve", lambda e: e.tensor_tensor(out=aqT[q2][:], in0=env.pb[PK_][:, 256:512].rearrange("p (a l) -> p a l", l=128), in1=Eall[q][:, 2:4, :], op=ALU.mult),
                      reads=[bEall[q]], xreads=[env.bpb[PK_]], writes=[baqT[q2]])
                yield
                for j in range(2):
                    mk.op("pe", lambda e: e.transpose(out=env.pb[PT_][:, j * 128:(j + 1) * 128], in_=PP[q][0][:, 2 + j, :], identity=env.cF(C_ID)),
                          reads=[bPP[q][0], env.bconst], writes=[env.bpb[PT_]])
                mk.op("act", lambda e: e.copy(out=PP[q][0][:, 0:2, :].rearrange("p a l -> p (a l)"), in_=env.pb[PT_][:, 0:256]), xreads=[env.bpb[PT_]], writes=[bPP[q][0]])
                yield
                mk.op("dve", lambda e: e.tensor_tensor(out=RR[q][0][:], in0=PP[q][0][:, 0:2, :], in1=bc(env.cF(C_ID), [128, 2, 128], 1), op=ALU.add),
                      reads=[bPP[q][0], env.bconst], writes=[bRR[q][0]])
                yield
                for k in range(1, 6):
                    src, dst = PP[q][(k - 1) % 2], PP[q][k % 2]
                    bsrc, bdst = bPP[q][(k - 1) % 2], bPP[q][k % 2]
                    for j in range(2):
                        if k < 5:
                            mk.op("pe", lambda e: e.matmul(env.pb[PT_][:, j * 128:(j + 1) * 128], lhsT=src[:, 2 + j, :], rhs=src[:, j, :], start=True, stop=True),
                                  reads=[bsrc], writes=[env.bpb[PT_]])
                        mk.op("pe", lambda e: e.matmul(env.pb[PT_][:, 256 + j * 128:256 + (j + 1) * 128], lhsT=src[:, j, :], rhs=src[:, 2 + j, :], start=True, stop=True),
                              reads=[bsrc], writes=[env.bpb[PT_]])
                    lo = 0 if k < 5 else 256
                    mk.op("act", lambda e: e.copy(out=dst[:].rearrange("p a l -> p (a l)")[:, lo:512], in_=env.pb[PT_][:, lo:512]), xreads=[env.bpb[PT_]], writes=[bdst])
                    yield
                    rs_, rd_ = RR[q][(k - 1) % 2], RR[q][k % 2]
                    brs_, brd_ = bRR[q][(k - 1) % 2], bRR[q][k % 2]
                    for j in range(2):
                        mk.op("pe", lambda e: e.matmul(env.pb[PD_][:, j * 128:(j + 1) * 128], lhsT=dst[:, 2 + j, :], rhs=rs_[:, j, :], start=True, stop=True),
                              reads=[bdst, brs_], writes=[env.bpb[PD_]])
                    mk.op("dve", lambda e: e.tensor_tensor(out=rd_[:], in0=rs_[:], in1=env.pb[PD_][:, 0:256].rearrange("p (a l) -> p a l", l=128), op=ALU.add),
                          reads=[brs_], xreads=[env.bpb[PD_]], writes=[brd_])
                    yield
                yield
                TT, bTT = RR[q][1], bRR[q][1]
                for j in range(2):
                    mk.op("pe", lambda e: e.matmul(env.pb[PT_][:, j * 128:(j + 1) * 128], lhsT=TT[:, j, :], rhs=bv[q][:, j, :], start=True, stop=True),
                          reads=[bTT, bbv[q]], writes=[env.bpb[PT_]])
                    mk.op("pe", lambda e: e.matmul(env.pb[PT_][:, 256 + j * 128:256 + (j + 1) * 128], lhsT=kbg[q][:, j, :], rhs=TT[:, j, :], start=True, stop=True),
                          reads=[bTT, bkbg[q]], writes=[env.bpb[PT_]])
                mk.op("act", lambda e: e.copy(out=u[q2][:].rearrange("p a l -> p (a l)"), in_=env.pb[PT_][:, 0:256]), xreads=[env.bpb[PT_]], writes=[bu[q2]])
                mk.op("act", lambda e: e.copy(out=wT[q2][:].rearrange("p a l -> p (a l)"), in_=env.pb[PT_][:, 256:512]), xreads=[env.bpb[PT_]], writes=[bwT[q2]])
                yield
                pre_done[d] = step + 1

        def chain(d):
            PK_, CB_ = 3 * d + 1, 6 + d
            for step in range(NBLK):
                while pre_done[d] <= step:
                    yield
                par = step % 2
                q2 = 2 * d + par
                b = step if d == 0 else NBLK - 1 - step
                cols = slice(b * 128, (b + 1) * 128)
                q = d
                d2 = slice(d * 2, d * 2 + 2)
                qT = [pc[0][:, cols], pc[3][:, cols]]; kT = [pc[1][:, cols], pc[4][:, cols]]; vT = [pc[2][:, cols], pc[5][:, cols]]
                bq = [pcB[0], pcB[3]]; bk = [pcB[1], pcB[4]]; bvv = [pcB[2], pcB[5]]
                for hf in ([0, 1] if d == 0 else [1, 0]):
                    rows = slice(hf * 64, hf * 64 + 64)
                    for j in range(2):
                        mk.op("pe", lambda e: e.matmul(env.pb[CB_][:, j * 128:(j + 1) * 128], lhsT=wT[q2][:, j, :], rhs=Sb[d][:, j, :], start=True, stop=True),
                              reads=[bwT[q2], bSb[d]], writes=[env.bpb[CB_]])
                    mk.op("dve", lambda e: e.tensor_tensor(out=vnew[q][rows, :, :], in0=u[q2][rows, :, :], in1=env.pb[CB_][rows, 0:256].rearrange("p (a l) -> p a l", l=128), op=ALU.subtract),
                          reads=[bu[q2]], xreads=[env.bpb[CB_]], writes=[bvnew[q]])
                    yield
                    for j in range(2):
                        mk.op("pe", lambda e: e.matmul(env.pb[PK_][:, j * 128:(j + 1) * 128], lhsT=qT[j], rhs=Sb[d][:, j, :], start=True, stop=True),
                              reads=[bq[j], bSb[d]], writes=[env.bpb[PK_]])
                        mk.op("pe", lambda e: e.matmul(env.pb[PK_][:, 256 + j * 128:256 + (j + 1) * 128], lhsT=aqT[q2][rows, j, :], rhs=vnew[q][rows, j, :], start=True, stop=True),
                              reads=[baqT[q2], bvnew[q]], writes=[env.bpb[PK_]])
                        mk.op("pe", lambda e: e.matmul(env.pb[CB_][:, 256 + j * 128:256 + (j + 1) * 128], lhsT=ke[q2][rows, j, :], rhs=vnew[q][rows, j, :], start=True, stop=True),
                              reads=[bke[q2], bvnew[q]], writes=[env.bpb[CB_]])
                    yield
                    for j in range(2):
                        mk.op("dve", lambda e: e.scalar_tensor_tensor(out=Sf[d][:, j, :], in0=Sf[d][:, j, :], scalar=ld[:, b, hf, d * 2 + j:d * 2 + j + 1],
                                                                      in1=env.pb[CB_][:, 256 + j * 128:256 + (j + 1) * 128], op0=ALU.mult, op1=ALU.add),
                              reads=[bld], xreads=[env.bpb[CB_]], writes=[bSf[d]])
                    mk.op("act", lambda e: e.copy(out=Sb[d][:], in_=Sf[d][:]), reads=[bSf[d]], writes=[bSb[d]])
                    for j in range(2):
                        mk.op("act", lambda e: e.activation(out=tt[q][rows, j, :], in_=env.pb[PK_][rows, j * 128:(j + 1) * 128], func=AF.Copy, scale=eg[rows, b, d * 2 + j:d * 2 + j + 1]),
                              reads=[beg_], xreads=[env.bpb[PK_]], writes=[btt[q]])
                    yield
                    o2 = env.pb[PK_][rows, 256:512].rearrange("p (a l) -> p a l", l=128)
                    oa = oacc[rows, b, :].rearrange("p (a l) -> p a l", l=128)
                    if not touched[b][hf]:
                        touched[b][hf] = True
                        mk.op("dve", lambda e: e.tensor_tensor(out=oa, in0=o2, in1=tt[q][rows, :, :], op=ALU.add), reads=[btt[q]], xreads=[env.bpb[PK_]], writes=[boacc[b][hf]])
                    else:
                        mk.op("dve", lambda e: e.tensor_tensor(out=t2[q][rows, :, :], in0=o2, in1=tt[q][rows, :, :], op=ALU.add), reads=[btt[q]], xreads=[env.bpb[PK_]], writes=[bt2[q]])
                        mk.op("dve", lambda e: e.tensor_tensor(out=oa, in0=oa, in1=t2[q][rows, :, :], op=ALU.add), reads=[bt2[q]], writes=[boacc[b][hf]])
                yield
                chain_done[d] = step + 1
        alive = [pre(0), pre(1), chain(0), chain(1)]
        while alive:
            for gnr in list(alive):
                try:
                    next(gnr)
                except StopIteration:
                    alive.remove(gnr)
        stg = [bft(f"gstg{i}", [128, 512]) for i in range(2)]
        bstg = [mk.buf() for _ in range(2)]
        it = 0
        for g4 in range(NBLK // 4):
            for j in range(2):
                q = it % 2
                bi = 6 + (it % 2)
                it += 1
                tps = env.pb[bi][:].bitcast(BF16)
                for jj in range(4):
                    b = g4 * 4 + jj
                    mk.op("pe", lambda e: e.transpose(out=tps[:, jj * 128:(jj + 1) * 128], in_=oacc[:, b, j * 128:(j + 1) * 128], identity=env.cB(C_ID)),
                          reads=[boacc[b][0], boacc[b][1], env.bconst], writes=[env.bpb[bi]])
                mk.op("act", lambda e: e.copy(out=stg[q][:], in_=tps[:, 0:512]), xreads=[env.bpb[bi]], writes=[bstg[q]])
                mk.dma("sp", ymb[2 + j, :, g4 * 512:(g4 + 1) * 512], stg[q][:], "ymb", reads=[bstg[q]], writes=[bymb])
        mk.barrier()


import numpy as np
import ml_dtypes
BF = ml_dtypes.bfloat16

def mb_cols(r):
    g = r // 2
    cols = []
    cols += list(range(2048 + 256 * r, 2048 + 256 * r + 256))
    cols += list(range(4096 + 128 * g, 4096 + 128 * g + 128))
    cols += list(range(4608 + 128 * g, 4608 + 128 * g + 128))
    for j in range(2):
        hd = 2 * r + j
        cols += list(range(5184 + 128 * hd, 5184 + 128 * hd + 128))
        cols += list(range(5184 + 2048 + 128 * hd, 5184 + 2048 + 128 * hd + 128))
        cols += list(range(5184 + 4096 + 128 * hd, 5184 + 4096 + 128 * hd + 128))
    for d in range(2):
        cols += [5120 + d * 32 + 4 * r + h for h in range(4)]
    for ab in range(2):
        for d in range(2):
            cols += [13376 + ab * 32 + d * 16 + 2 * r + j for j in range(2)]
    return np.array(cols)

def mb_conv(r, ssd_w, ssd_b, gdn_w):
    g = r // 2
    cw = np.zeros((128, 10, 6), np.float32)
    def put(ci, w, b, ch0):
        cw[:, ci, 0:5] = w[:, ch0:ch0 + 128].T
        if b is not None:
            cw[:, ci, 5] = b[ch0:ch0 + 128]
    put(0, ssd_w, ssd_b, 256 * r); put(1, ssd_w, ssd_b, 256 * r + 128)
    put(2, ssd_w, ssd_b, 2048 + 128 * g); put(3, ssd_w, ssd_b, 2560 + 128 * g)
    for j in range(2):
        hd = 2 * r + j
        put(4 + 3 * j, gdn_w, None, 128 * hd); put(5 + 3 * j, gdn_w, None, 2048 + 128 * hd); put(6 + 3 * j, gdn_w, None, 4096 + 128 * hd)
    return cw

def mb_sp(r, ssd_dt_bias, ssd_a_log, ssd_d, gdn_dt_bias, gdn_a_log):
    v = np.zeros(32, np.float32)
    for d in range(2):
        for h in range(4):
            v[d * 4 + h] = ssd_dt_bias[d, 4 * r + h]; v[8 + d * 4 + h] = ssd_a_log[d, 4 * r + h]
        for j in range(2):
            v[20 + d * 2 + j] = gdn_dt_bias[d, 2 * r + j]; v[24 + d * 2 + j] = gdn_a_log[d, 2 * r + j]
    for h in range(4):
        v[16 + h] = ssd_d[4 * r + h]
    return np.ascontiguousarray(np.broadcast_to(v, (128, 32)))

def to_fm(x, kc=16):
    return np.ascontiguousarray(x.T.reshape(kc, 128, -1).transpose(1, 0, 2))

def from_fm(xT):
    return xT.transpose(1, 0, 2).reshape(xT.shape[0] * xT.shape[1], -1).T

def col(g, kc=16):
    return np.ascontiguousarray(g.reshape(kc, 128).T)


_PROGS = {}


def build_p1():
    nc = bass.Bass("TRN2", target_bir_lowering=False)
    xT = nc.dram_tensor("xT", [128, KC, NT], F32, kind="ExternalInput").ap()
    w_gu = nc.dram_tensor("w_gu", [D, 2 * DFF], F32, kind="ExternalInput").ap()
    w_down = nc.dram_tensor("w_down", [DFF, D], F32, kind="ExternalInput").ap()
    gv = nc.dram_tensor("gv", [128, 3 * KC], F32, kind="ExternalInput").ap()
    consts = nc.dram_tensor("consts", [128, NCONST * 128], F32, kind="ExternalInput").ap()
    x1T = nc.dram_tensor("x1T", [128, KC, NT], F32, kind="ExternalOutput").ap()
    hTo = nc.dram_tensor("hTo", [128, KC, NT], BF16, kind="ExternalOutput").ap()
    with ExitStack() as es:
        mk = MK(nc, es)
        env = Env(nc, mk, es, consts)
        g = env.sb(es, "gv", [128, 3 * KC], F32)
        bg = mk.buf()
        mk.dma("sp", g[:], gv, "gv", writes=[bg])
        mk.op("act", lambda e: e.mul(out=g[:, KC:2 * KC], in_=g[:, KC:2 * KC], mul=0.5), reads=[bg], writes=[bg])
        bx1, bh = mk.buf(), mk.buf()
        phase_ffn(env, xT, x1T, bx1, w_gu, w_down, g[:, 0:KC], g[:, KC:2 * KC], bg)
        mk.barrier()
        phase_ma(env, x1T, g[:, 2 * KC:3 * KC], bg, hTo, bh)
        mk.barrier()
    return nc


def build_p2():
    nc = bass.Bass("TRN2", target_bir_lowering=False)
    hTall = nc.dram_tensor("hTall", [128, KC, SEQ], BF16, kind="ExternalInput").ap()
    wmb = nc.dram_tensor("wmb", [D, 1296], F32, kind="ExternalInput").ap()
    cwd = nc.dram_tensor("cw", [128, 10, 6], F32, kind="ExternalInput").ap()
    spd = nc.dram_tensor("sp", [128, 32], F32, kind="ExternalInput").ap()
    consts = nc.dram_tensor("consts", [128, NCONST * 128], F32, kind="ExternalInput").ap()
    ymb = nc.dram_tensor("ymb", [4, 128, SEQ], BF16, kind="ExternalOutput").ap()
    with ExitStack() as es:
        mk = MK(nc, es)
        env = Env(nc, mk, es, consts)
        phase_mb(env, es, hTall, wmb, cwd, spd, ymb, do_gdn=True)
        mk.barrier()
    return nc


def build_p3():
    nc = bass.Bass("TRN2", target_bir_lowering=False)
    x1T = nc.dram_tensor("x1T", [128, KC, NT], F32, kind="ExternalInput").ap()
    hTi = nc.dram_tensor("hTi", [128, KC, NT], BF16, kind="ExternalInput").ap()
    yTi = nc.dram_tensor("yTi", [128, 32, NT], BF16, kind="ExternalInput").ap()
    w_z = nc.dram_tensor("w_z", [D, 4096], F32, kind="ExternalInput").ap()
    w_out = nc.dram_tensor("w_out", [4096, D], F32, kind="ExternalInput").ap()
    gsd = nc.dram_tensor("gs", [128, 33], F32, kind="ExternalInput").ap()
    w_gu = nc.dram_tensor("w_gu", [D, 2 * DFF], F32, kind="ExternalInput").ap()
    w_down = nc.dram_tensor("w_down", [DFF, D], F32, kind="ExternalInput").ap()
    gv = nc.dram_tensor("gv", [128, 2 * KC], F32, kind="ExternalInput").ap()
    consts = nc.dram_tensor("consts", [128, NCONST * 128], F32, kind="ExternalInput").ap()
    x2T = nc.dram_tensor("x2T", [128, KC, NT], F32).ap()
    x3T = nc.dram_tensor("x3T", [128, KC, NT], F32, kind="ExternalOutput").ap()
    with ExitStack() as es:
        mk = MK(nc, es)
        env = Env(nc, mk, es, consts)
        g = env.sb(es, "gv", [128, 2 * KC], F32)
        gs = env.sb(es, "gs", [128, 33], F32)
        bg = mk.buf()
        mk.dma("sp", g[:], gv, "gv", writes=[bg])
        mk.dma("sp", gs[:], gsd, "gv", writes=[bg])
        mk.op("act", lambda e: e.mul(out=g[:, KC:2 * KC], in_=g[:, KC:2 * KC], mul=0.5), reads=[bg], writes=[bg])
        bx2, bx3 = mk.buf(), mk.buf()
        phase_mc(env, x1T, x2T, bx2, hTi, yTi, w_z, w_out, gs, bg, zcols=[i * 128 for i in range(32)])
        mk.barrier()
        phase_ffn(env, x2T, x3T, bx3, w_gu, w_down, g[:, 0:KC], g[:, KC:2 * KC], bg)
        mk.barrier()
    return nc


def _prog(name):
    if name not in _PROGS:
        _PROGS[name] = {"p1": build_p1, "p2": build_p2, "p3": build_p3}[name]()
    return _PROGS[name]


ZCOLS = np.concatenate([np.arange(0, 2048), np.arange(11328, 11328 + 2048)])


def kernel_unfused(**inp):
    f32 = np.float32
    x = np.asarray(inp["x"], f32)[0]
    consts = make_consts()
    cores = list(range(NCORES))
    xT = [to_fm(x[r * NT:(r + 1) * NT]) for r in cores]
    for L in range(4):
        gI = lambda k: np.asarray(inp[k], f32)[L]
        gv = np.concatenate([col(gI("ffn1_pre_g")), col(gI("ffn1_post_g")), col(gI("mix_pre_g"))], 1)
        wgu, wd = np.ascontiguousarray(gI("ffn1_w_gu")), np.ascontiguousarray(gI("ffn1_w_down"))
        res = run_bass_kernel_spmd(_prog("p1"), [{"xT": xT[r], "w_gu": wgu, "w_down": wd, "gv": gv, "consts": consts} for r in cores], core_ids=cores)
        x1T = [res.results[r]["x1T"] for r in cores]
        hT = [res.results[r]["hTo"] for r in cores]
        hTall = np.ascontiguousarray(np.concatenate(hT, axis=2))
        w_in = gI("w_in")
        im = []
        for r in cores:
            im.append({"hTall": hTall, "wmb": np.ascontiguousarray(w_in[:, mb_cols(r)]),
                       "cw": mb_conv(r, gI("ssd_conv_w"), gI("ssd_conv_b"), gI("gdn_conv_w")),
                       "sp": mb_sp(r, gI("ssd_dt_bias"), gI("ssd_a_log"), gI("ssd_d"), gI("gdn_dt_bias"), gI("gdn_a_log")),
                       "consts": consts})
        res = run_bass_kernel_spmd(_prog("p2"), im, core_ids=cores)
        yall = np.empty((32, 128, SEQ), dtype=res.results[0]["ymb"].dtype)
        for r in cores:
            y = res.results[r]["ymb"]
            yall[2 * r] = y[0]; yall[2 * r + 1] = y[1]; yall[16 + 2 * r] = y[2]; yall[16 + 2 * r + 1] = y[3]
        gs = np.concatenate([col(gI("ssd_norm_g")), gI("gdn_norm_g").reshape(128, 1), col(gI("mix_post_g"))], 1)
        gv2 = np.concatenate([col(gI("ffn2_pre_g")), col(gI("ffn2_post_g"))], 1)
        w_z = np.ascontiguousarray(w_in[:, ZCOLS])
        w_out = np.ascontiguousarray(gI("w_out"))
        wgu2, wd2 = np.ascontiguousarray(gI("ffn2_w_gu")), np.ascontiguousarray(gI("ffn2_w_down"))
        im = []
        for r in cores:
            yT = np.ascontiguousarray(yall[:, :, r * NT:(r + 1) * NT].transpose(1, 0, 2))
            im.append({"x1T": x1T[r], "hTi": hT[r], "yTi": yT, "w_z": w_z, "w_out": w_out, "gs": gs,
                       "w_gu": wgu2, "w_down": wd2, "gv": gv2, "consts": consts})
        res = run_bass_kernel_spmd(_prog("p3"), im, core_ids=cores)
        xT = [res.results[r]["x3T"] for r in cores]
    out = np.concatenate([from_fm(np.asarray(xT[r], f32)) for r in cores], axis=0)
    return np.ascontiguousarray(out[None]).astype(f32)


NG = 5 * KC + 33
DEPTH = 4


def build_fused(depth=DEPTH):
    nc = bass.Bass("TRN2", target_bir_lowering=False)
    dt_in = lambda name, shape, dt=F32: nc.dram_tensor(name, shape, dt, kind="ExternalInput").ap()
    xT = dt_in("xT", [128, KC, NT])
    w_gu1 = dt_in("ffn1_w_gu", [depth, D, 2 * DFF]); w_dn1 = dt_in("ffn1_w_down", [depth, DFF, D])
    w_gu2 = dt_in("ffn2_w_gu", [depth, D, 2 * DFF]); w_dn2 = dt_in("ffn2_w_down", [depth, DFF, D])
    w_in = dt_in("w_in", [depth, D, 13440]); w_out = dt_in("w_out", [depth, 4096, D])
    wmb = dt_in("wmb", [depth, D, 1296]); cwd = dt_in("cw", [depth, 128, 10, 6]); spd = dt_in("sp", [depth, 128, 32])
    gall = dt_in("gall", [128, depth, NG])
    consts = dt_in("consts", [128, NCONST * 128])
    xo = nc.dram_tensor("xo", [128, KC, NT], F32, kind="ExternalOutput").ap()
    with ExitStack() as es:
        mk = MK(nc, es)
        env = Env(nc, mk, es, consts)
        g = env.sb(es, "gall", [128, depth, NG], F32)
        bg = mk.buf()
        mk.dma("sp", g[:], gall, "gv", writes=[bg])
        for L in range(depth):
            for o in (KC, 4 * KC):
                mk.op("act", lambda e: e.mul(out=g[:, L, o:o + KC], in_=g[:, L, o:o + KC], mul=0.5), reads=[bg], writes=[bg])
        xcur = xT
        for L in range(depth):
            x1 = nc.dram_tensor(f"x1_{L}", [128, KC, NT], F32).ap()
            x2 = nc.dram_tensor(f"x2_{L}", [128, KC, NT], F32).ap()
            x3 = xo if L == depth - 1 else nc.dram_tensor(f"x3_{L}", [128, KC, NT], F32).ap()
            hloc = nc.dram_tensor(f"hloc_{L}", [128 * KC, NT], BF16).ap()
            hall = nc.dram_tensor(f"hall_{L}", [NCORES * 128 * KC, NT], BF16).ap()
            yloc = nc.dram_tensor(f"yloc_{L}", [4 * 128, SEQ], BF16).ap()
            yall = nc.dram_tensor(f"yall_{L}", [NCORES * 4 * 128, SEQ], BF16).ap()
            gl = g[:, L, :]
            bx1, bx2, bx3, bh, bhall, by, byall = [mk.buf() for _ in range(7)]
            phase_ffn(env, xcur, x1, bx1, w_gu1[L], w_dn1[L], gl[:, 0:KC], gl[:, KC:2 * KC], bg)
            mk.barrier()
            hloc3 = hloc.rearrange("(p k) t -> p k t", k=KC)
            phase_ma(env, x1, gl[:, 2 * KC:3 * KC], bg, hloc3, bh)
            mk.collective("AllGather", hloc, hall, "cc", reads=[bh], writes=[bhall])
            def ht_src(T, hall=hall):
                rr, off = (T * 512) // NT, (T * 512) % NT
                return hall[rr * 128 * KC:(rr + 1) * 128 * KC, off:off + 512].rearrange("(p k) t -> p k t", k=KC)
            yloc3 = yloc.rearrange("(a p) t -> a p t", p=128)
            with ExitStack() as esL:
                phase_mb(env, esL, ht_src, wmb[L], cwd[L], spd[L], yloc3, do_gdn=True, hT_reads=[bhall], bymb=by)
            mk.collective("AllGather", yloc, yall, "cc", reads=[by], writes=[byall])
            yslab = nc.dram_tensor(f"yslab_{L}", [NCORES * 4 * 128, NT], BF16).ap()
            byslab = mk.buf()
            rank = nc.sync.partition_id()
            mk.dma("sp", yslab, yall[:, bass.ds(rank * NT, NT)], "yslab", reads=[byall], writes=[byslab])

            def y_src(i, yslab=yslab):
                if i < 16:
                    r_, a_ = i // 2, i % 2
                else:
                    r_, a_ = (i - 16) // 2, 2 + (i - 16) % 2
                row0 = r_ * 512 + a_ * 128
                return yslab[row0:row0 + 128, :]
            phase_mc(env, x1, x2, bx2, hloc3, y_src, w_in[L], w_out[L], gl[:, 5 * KC:5 * KC + 33], bg, y_reads=[byslab])
            mk.barrier()
            phase_ffn(env, x2, x3, bx3, w_gu2[L], w_dn2[L], gl[:, 3 * KC:4 * KC], gl[:, 4 * KC:5 * KC], bg)
            mk.barrier()
            xcur = x3
        mk.barrier()
    print("fused program: instr counts", mk.cnt, "n dma sems", len(mk.dsems), flush=True)
    return nc


def kernel_fused(depth=DEPTH, **inp):
    f32 = np.float32
    x = np.asarray(inp["x"], f32)[0]
    consts = make_consts()
    cores = list(range(NCORES))
    A = lambda k: np.ascontiguousarray(np.asarray(inp[k], f32)[:depth])
    gall = np.zeros((128, depth, NG), f32)
    for L in range(depth):
        gI = lambda k: np.asarray(inp[k], f32)[L]
        gall[:, L, :] = np.concatenate([col(gI("ffn1_pre_g")), col(gI("ffn1_post_g")), col(gI("mix_pre_g")), col(gI("ffn2_pre_g")), col(gI("ffn2_post_g")),
                                        col(gI("ssd_norm_g")), gI("gdn_norm_g").reshape(128, 1), col(gI("mix_post_g"))], 1)
    shared = {"ffn1_w_gu": A("ffn1_w_gu"), "ffn1_w_down": A("ffn1_w_down"), "ffn2_w_gu": A("ffn2_w_gu"), "ffn2_w_down": A("ffn2_w_down"),
              "w_in": A("w_in"), "w_out": A("w_out"), "gall": gall, "consts": consts}
    w_in = shared["w_in"]
    im = []
    for r in cores:
        gL = lambda k, L: np.asarray(inp[k], f32)[L]
        d = dict(shared)
        d["xT"] = to_fm(x[r * NT:(r + 1) * NT])
        d["wmb"] = np.ascontiguousarray(w_in[:, :, mb_cols(r)])
        d["cw"] = np.stack([mb_conv(r, gL("ssd_conv_w", L), gL("ssd_conv_b", L), gL("gdn_conv_w", L)) for L in range(depth)])
        d["sp"] = np.stack([mb_sp(r, gL("ssd_dt_bias", L), gL("ssd_a_log", L), gL("ssd_d", L), gL("gdn_dt_bias", L), gL("gdn_a_log", L)) for L in range(depth)])
        im.append(d)
    if "fused" not in _PROGS:
        _PROGS["fused"] = build_fused(depth)
    res = run_bass_kernel_spmd(_PROGS["fused"], im, core_ids=cores)
    out = np.concatenate([from_fm(np.asarray(res.results[r]["xo"], f32)) for r in cores], axis=0)
    return np.ascontiguousarray(out[None]).astype(f32)


def kernel(**inp):
    return kernel_fused(depth=DEPTH, **inp)
```
